# Optimizing a Trainium2 kernel written in Bass

```python
import math
import jax
import jax.numpy as jnp
from jax import lax
import numpy as np

D_MODEL = 1024
BATCH = 8
SEQ = 4096
DEPTH = 4

CTX_LEN = 256
GRID_W = 64
EPS = 1e-6

MIX_WIDTH = D_MODEL
POOL_WIDTH = MIX_WIDTH // 4
GDN_WIDTH = MIX_WIDTH // 2
NA_WIDTH = MIX_WIDTH - POOL_WIDTH - GDN_WIDTH

POOL_WINDOWS = (2, 4, 8, 16)
POOL_GROUPS = len(POOL_WINDOWS)
POOL_GROUP_DIM = POOL_WIDTH // POOL_GROUPS

GDN_HEAD_DIM = 128
GDN_HEADS = GDN_WIDTH // GDN_HEAD_DIM
GDN_CHUNK = 64
SHORT_CONV = 5
ROPE_THETA = 10000.0

NA_HEAD_DIM = 64
NA_HEADS = NA_WIDTH // NA_HEAD_DIM
NA_WIN_ROWS = 8
NA_WIN_COLS = 16
NA_QB = 16
NA_KB_COLS = NA_QB + NA_WIN_COLS

D_FF = 2816
FFN_CONV = 3

POOL_END = POOL_WIDTH
GQKV_END = POOL_END + 3 * GDN_WIDTH
GZ_END = GQKV_END + GDN_WIDTH
GAB_END = GZ_END + 4 * GDN_HEADS
N_IN = GAB_END + 3 * NA_WIDTH

kernel_name = "hybrid_pool_gdn_natten_dit_trunk"


def rmsnorm(x, w):
    xf = x.astype(jnp.float32)
    y = xf * lax.rsqrt(jnp.mean(xf * xf, axis=-1, keepdims=True) + EPS)
    return (y * w.astype(jnp.float32)).astype(x.dtype)


def l2norm(x):
    return x * lax.rsqrt(jnp.sum(x * x, axis=-1, keepdims=True) + EPS)


def centred_dwconv(x, w):
    k = w.shape[0]
    return lax.conv_general_dilated(
        x, w[:, None, :].astype(x.dtype), window_strides=(1,),
        padding=[(k // 2, k // 2)], dimension_numbers=("NWC", "WIO", "NWC"),
        feature_group_count=x.shape[-1])


def merge_heads(o):
    b, h, l, d = o.shape
    return o.transpose(0, 2, 1, 3).reshape(b, l, h * d)


def multiscale_pool(v, w_pool, scale):
    L = v.shape[1]
    vf = v.astype(jnp.float32)
    csum = jnp.pad(jnp.cumsum(vf, axis=1), ((0, 0), (1, 0), (0, 0)))
    t = jnp.arange(L)
    outs = []
    for g, win in enumerate(POOL_WINDOWS):
        lo = jnp.clip(t - win // 2, 0, L)
        hi = jnp.clip(t + win - win // 2, 0, L)
        sl = slice(g * POOL_GROUP_DIM, (g + 1) * POOL_GROUP_DIM)
        seg = csum[:, :, sl]
        mean = (seg[:, hi] - seg[:, lo]) / (hi - lo).astype(jnp.float32)[None, :, None]
        outs.append(mean - vf[:, :, sl])
    pooled = jnp.stack(outs, axis=2).astype(v.dtype)
    y = jnp.einsum("blgc,gcd->blgd", pooled, w_pool)
    return y.reshape(v.shape) * scale


def axial_rope(x):
    L, dim = x.shape[2], x.shape[-1]
    half = dim // 2
    nf = half // 2
    t = jnp.arange(L)
    freqs = ROPE_THETA ** (-jnp.arange(nf, dtype=jnp.float32) / nf)

    def rot(xp, pos):
        ang = pos.astype(jnp.float32)[:, None] * freqs
        cos, sin = jnp.cos(ang), jnp.sin(ang)
        x1, x2 = xp[..., :nf], xp[..., nf:]
        return jnp.concatenate([x1 * cos - x2 * sin, x1 * sin + x2 * cos], axis=-1)

    return jnp.concatenate([rot(x[..., :half], t // GRID_W), rot(x[..., half:], t % GRID_W)], axis=-1)


def gdn_prepare(p_qkv, p_ab, conv_w, a_log, dt_bias, rope):
    B, L, _ = p_qkv.shape
    qkv = jax.nn.silu(centred_dwconv(p_qkv, conv_w)).astype(jnp.float32)
    q, k, v = jnp.split(qkv, 3, axis=-1)

    def heads(t):
        return t.reshape(B, L, GDN_HEADS, GDN_HEAD_DIM).transpose(0, 2, 1, 3)

    q, k, v = l2norm(heads(q)), l2norm(heads(k)), heads(v)
    if rope:
        q, k = axial_rope(q), axial_rope(k)
    ab = p_ab.astype(jnp.float32).reshape(B, L, 2, 2, GDN_HEADS)
    a, b = ab[:, :, 0], ab[:, :, 1]
    g = -jnp.exp(a_log.astype(jnp.float32)) * jax.nn.softplus(a + dt_bias.astype(jnp.float32))
    beta = jax.nn.sigmoid(b)
    return q, k, v, g.transpose(2, 0, 3, 1), beta.transpose(2, 0, 3, 1)


def gdn_chunked(q, k, v, g, beta, state):
    B, H, L, dk = q.shape
    dv = v.shape[-1]
    n = L // GDN_CHUNK

    def ch(t):
        return t.reshape(B, H, n, GDN_CHUNK, *t.shape[3:])

    q, k, v, g, beta = ch(q) * dk ** -0.5, ch(k), ch(v), ch(g), ch(beta)
    gc = jnp.cumsum(g, axis=-1)
    incl = np.tril(np.ones((GDN_CHUNK, GDN_CHUNK), dtype=bool))
    decay = jnp.where(incl, jnp.exp(jnp.where(incl, gc[..., :, None] - gc[..., None, :], 0.0)), 0.0)
    kk = jnp.einsum("bhncd,bhnsd->bhncs", k * beta[..., None], k) * decay
    a_mat = jnp.tril(kk, -1) + jnp.eye(GDN_CHUNK, dtype=kk.dtype)
    rhs = jnp.concatenate([v * beta[..., None], k * (beta * jnp.exp(gc))[..., None]], axis=-1)
    uw = lax.linalg.triangular_solve(a_mat, rhs, left_side=True, lower=True, unit_diagonal=True)
    u, w = uw[..., :dv], uw[..., dv:]
    intra = jnp.einsum("bhncd,bhnsd->bhncs", q, k) * decay
    q_dec = q * jnp.exp(gc)[..., None]
    k_dec = k * jnp.exp(gc[..., -1:] - gc)[..., None]
    g_last = jnp.exp(gc[..., -1])

    def step(S, xs):
        u_n, w_n, intra_n, qd_n, kd_n, gl_n = xs
        v_new = u_n - jnp.einsum("bhck,bhkv->bhcv", w_n, S)
        o_n = jnp.einsum("bhck,bhkv->bhcv", qd_n, S) + jnp.einsum("bhcs,bhsv->bhcv", intra_n, v_new)
        S = S * gl_n[..., None, None] + jnp.einsum("bhck,bhcv->bhkv", kd_n, v_new)
        return S, o_n

    xs = tuple(jnp.moveaxis(t, 2, 0) for t in (u, w, intra, q_dec, k_dec, g_last))
    state, o = lax.scan(step, state, xs)
    return jnp.moveaxis(o, 0, 2).reshape(B, H, L, dv), state


def gdn_bidirectional(ctx_in, lat_in):
    qc, kc, vc, gc, bc = ctx_in
    ql, kl, vl, gl, bl = lat_in
    B, H, _, dk = qc.shape
    s0 = jnp.zeros((B, H, dk, vc.shape[-1]), jnp.float32)

    def flip(t):
        return jnp.flip(t, axis=2)

    oc_f, sc_f = gdn_chunked(qc, kc, vc, gc[0], bc[0], s0)
    ol_f, _ = gdn_chunked(ql, kl, vl, gl[0], bl[0], sc_f)
    oc_b, sc_b = gdn_chunked(flip(qc), flip(kc), flip(vc), flip(gc[1]), flip(bc[1]), s0)
    ol_b, _ = gdn_chunked(flip(ql), flip(kl), flip(vl), flip(gl[1]), flip(bl[1]), sc_b)
    return oc_f + flip(oc_b), ol_f + flip(ol_b)


def gdn_output(o, z, norm_w):
    B, H, L, dv = o.shape
    o = o.transpose(0, 2, 1, 3)
    on = o * lax.rsqrt(jnp.mean(o * o, axis=-1, keepdims=True) + EPS) * norm_w.astype(jnp.float32)
    zf = z.astype(jnp.float32).reshape(B, L, H, dv)
    return (on * jax.nn.silu(zf)).reshape(B, L, H * dv).astype(z.dtype)


def na_heads(p, q_norm, k_norm):
    B, L, _ = p.shape
    q, k, v = jnp.split(p, 3, axis=-1)

    def heads(t):
        return t.reshape(B, L, NA_HEADS, NA_HEAD_DIM).transpose(0, 2, 1, 3)

    return rmsnorm(heads(q), q_norm), rmsnorm(heads(k), k_norm), heads(v)


def na_col_tables():
    nb = GRID_W // NA_QB
    q_cols = np.arange(nb)[:, None] * NA_QB + np.arange(NA_QB)[None, :]
    k_start = np.clip(np.arange(nb) * NA_QB - NA_WIN_COLS // 2, 0, GRID_W - NA_KB_COLS)
    k_cols = k_start[:, None] + np.arange(NA_KB_COLS)[None, :]
    c0 = np.clip(q_cols - NA_WIN_COLS // 2, 0, GRID_W - NA_WIN_COLS)
    kc = k_cols[:, None, :]
    mask = (kc >= c0[..., None]) & (kc < c0[..., None] + NA_WIN_COLS)
    dc = np.clip(kc - q_cols[..., None] + NA_WIN_COLS - 1, 0, 2 * NA_WIN_COLS - 2)
    return k_cols, mask, dc


def neighbourhood_attention(q, k, v, kc, vc, rpb):
    B, H, L, dh = q.shape
    rows = L // GRID_W
    kr = min(NA_WIN_ROWS, rows)
    nb = GRID_W // NA_QB
    nk = kr * NA_KB_COLS
    k_cols, mask, dc = na_col_tables()
    key_mask = np.broadcast_to(mask[:, :, None, :], (nb, NA_QB, kr, NA_KB_COLS)).reshape(nb, NA_QB, nk)
    col_bias = rpb[:, :, dc]
    scale = dh ** -0.5

    def grid(t):
        return t.reshape(B, H, rows, GRID_W, dh)

    qg, kg, vg = grid(q), grid(k), grid(v)

    def band(t, r0):
        t = lax.dynamic_slice_in_dim(t, r0, kr, axis=2)[:, :, :, k_cols]
        return t.transpose(0, 1, 3, 2, 4, 5).reshape(B, H, nb, nk, dh)

    def row(r):
        r0 = jnp.clip(r - kr // 2, 0, rows - kr)
        kb, vb = band(kg, r0), band(vg, r0)
        qr = lax.dynamic_index_in_dim(qg, r, axis=2, keepdims=False).reshape(B, H, nb, NA_QB, dh)
        dr = r0 + jnp.arange(kr) - r + NA_WIN_ROWS - 1
        bias = jnp.take(col_bias, dr, axis=1).transpose(0, 2, 3, 1, 4).reshape(H, nb, NA_QB, nk)
        s_loc = jnp.einsum("bhnqd,bhnkd->bhnqk", qr, kb).astype(jnp.float32) * scale + bias
        s_loc = jnp.where(key_mask, s_loc, -jnp.inf)
        s_ctx = jnp.einsum("bhnqd,bhkd->bhnqk", qr, kc).astype(jnp.float32) * scale
        p = jax.nn.softmax(jnp.concatenate([s_loc, s_ctx], axis=-1), axis=-1).astype(v.dtype)
        o = (jnp.einsum("bhnqk,bhnkd->bhnqd", p[..., :nk], vb)
             + jnp.einsum("bhnqk,bhkd->bhnqd", p[..., nk:], vc))
        return o.reshape(B, H, GRID_W, dh)

    out = lax.map(row, jnp.arange(rows))
    return out.transpose(1, 2, 0, 3, 4).reshape(B, H, L, dh)


def context_attention(q, k, v):
    s = jnp.einsum("bhqd,bhkd->bhqk", q, k).astype(jnp.float32) * q.shape[-1] ** -0.5
    p = jax.nn.softmax(s, axis=-1).astype(v.dtype)
    return jnp.einsum("bhqk,bhkd->bhqd", p, v)


def token_mixers(p, pc, pool_w, pool_scale, conv_w, a_log, dt_bias, gdn_norm_w,
                 q_norm, k_norm, rpb, with_ctx_out):
    splits = [POOL_END, GQKV_END, GZ_END, GAB_END]
    pv, gqkv, gz, gab, na = jnp.split(p, splits, axis=-1)
    pvc, gqkvc, gzc, gabc, nac = jnp.split(pc, splits, axis=-1)
    y_a = multiscale_pool(pv, pool_w, pool_scale)
    lat_b = gdn_prepare(gqkv, gab, conv_w, a_log, dt_bias, rope=True)
    ctx_b = gdn_prepare(gqkvc, gabc, conv_w, a_log, dt_bias, rope=False)
    o_ctx_b, o_lat_b = gdn_bidirectional(ctx_b, lat_b)
    y_b = gdn_output(o_lat_b, gz, gdn_norm_w)
    q, k, v = na_heads(na, q_norm, k_norm)
    qc, kc, vc = na_heads(nac, q_norm, k_norm)
    y_c = merge_heads(neighbourhood_attention(q, k, v, kc, vc, rpb))
    y = jnp.concatenate([y_a, y_b, y_c], axis=-1)
    if not with_ctx_out:
        return y, None
    yc_a = multiscale_pool(pvc, pool_w, pool_scale)
    yc_b = gdn_output(o_ctx_b, gzc, gdn_norm_w)
    yc_c = merge_heads(context_attention(qc, kc, vc))
    return y, jnp.concatenate([yc_a, yc_b, yc_c], axis=-1)


def conv_ffn(h, w_up, conv_w, w_down):
    u = centred_dwconv(h @ w_up, conv_w)
    a, b = jnp.split(u, 2, axis=-1)
    return (jax.nn.silu(a) * b) @ w_down


def setup_inputs(seed: int = 0) -> dict:
    key = jax.random.key(seed)
    ks = jax.random.split(key, 22)
    f32 = jnp.float32

    def nrm(k, shape, std):
        return jax.random.normal(k, shape, f32) * std

    def gain(k, shape):
        return 1.0 + nrm(k, shape, 0.02)

    L, D = DEPTH, D_MODEL
    dt = jnp.exp(jax.random.uniform(ks[12], (L, 2, GDN_HEADS), f32, math.log(1e-3), math.log(1e-1)))
    return {
        "x": nrm(ks[0], (BATCH, SEQ, D), 1.0),
        "c": nrm(ks[1], (BATCH, D), 1.0),
        "ctx": nrm(ks[2], (BATCH, CTX_LEN, D), 1.0),
        "c_ctx": nrm(ks[3], (D,), 1.0),
        "w_ada": nrm(ks[4], (L, D, 6 * D), 0.5 * D ** -0.5),
        "b_ada": nrm(ks[5], (L, 6 * D), 0.01),
        "norm_mix": gain(ks[6], (L, D)),
        "w_in": nrm(ks[7], (L, D, N_IN), D ** -0.5),
        "pool_w": nrm(ks[8], (L, POOL_GROUPS, POOL_GROUP_DIM, POOL_GROUP_DIM), POOL_GROUP_DIM ** -0.5),
        "pool_scale": gain(ks[9], (L, POOL_WIDTH)),
        "gdn_conv": nrm(ks[10], (L, SHORT_CONV, 3 * GDN_WIDTH), SHORT_CONV ** -0.5),
        "gdn_a_log": jnp.log(jax.random.uniform(ks[11], (L, 2, GDN_HEADS), f32, 1.0, 16.0)),
        "gdn_dt_bias": dt + jnp.log(-jnp.expm1(-dt)),
        "gdn_norm": gain(ks[13], (L, GDN_HEAD_DIM)),
        "na_q_norm": gain(ks[14], (L, NA_HEAD_DIM)),
        "na_k_norm": gain(ks[15], (L, NA_HEAD_DIM)),
        "na_rpb": nrm(ks[16], (L, NA_HEADS, 2 * NA_WIN_ROWS - 1, 2 * NA_WIN_COLS - 1), 0.1),
        "w_out": nrm(ks[17], (L, MIX_WIDTH, D), MIX_WIDTH ** -0.5),
        "norm_ffn": gain(ks[18], (L, D)),
        "w_up": nrm(ks[19], (L, D, 2 * D_FF), D ** -0.5),
        "ffn_conv": nrm(ks[20], (L, FFN_CONV, 2 * D_FF), FFN_CONV ** -0.5),
        "w_down": nrm(ks[21], (L, D_FF, D), D_FF ** -0.5),
    }


def reference(x, c, ctx, c_ctx, w_ada, b_ada, norm_mix, w_in, pool_w, pool_scale, gdn_conv,
              gdn_a_log, gdn_dt_bias, gdn_norm, na_q_norm, na_k_norm, na_rpb, w_out,
              norm_ffn, w_up, ffn_conv, w_down):
    sc = jax.nn.silu(c)
    scc = jax.nn.silu(c_ctx)
    for l in range(DEPTH):
        with_ctx_out = l < DEPTH - 1
        mod = (sc @ w_ada[l] + b_ada[l])[:, None, :]
        mod_c = scc @ w_ada[l] + b_ada[l]
        sh1, s1, g1, sh2, s2, g2 = jnp.split(mod, 6, axis=-1)
        csh1, cs1, cg1, csh2, cs2, cg2 = jnp.split(mod_c, 6, axis=-1)
        h = rmsnorm(x, norm_mix[l]) * (1.0 + s1) + sh1
        hc = rmsnorm(ctx, norm_mix[l]) * (1.0 + cs1) + csh1
        y, yc = token_mixers(h @ w_in[l], hc @ w_in[l], pool_w[l], pool_scale[l], gdn_conv[l],
                             gdn_a_log[l], gdn_dt_bias[l], gdn_norm[l], na_q_norm[l],
                             na_k_norm[l], na_rpb[l], with_ctx_out)
        x = x + g1 * (y @ w_out[l])
        hf = rmsnorm(x, norm_ffn[l]) * (1.0 + s2) + sh2
        x = x + g2 * conv_ffn(hf, w_up[l], ffn_conv[l], w_down[l])
        if with_ctx_out:
            ctx = ctx + cg1 * (yc @ w_out[l])
            hfc = rmsnorm(ctx, norm_ffn[l]) * (1.0 + cs2) + csh2
            ctx = ctx + cg2 * conv_ffn(hfc, w_up[l], ffn_conv[l], w_down[l])
    return x
```

```python
import math
from contextlib import ExitStack, contextmanager

import ml_dtypes
import numpy as np

import concourse.bass as bass
import concourse.mybir as mybir
from concourse.bass_utils import run_bass_kernel_spmd

F32 = mybir.dt.float32
BF16 = mybir.dt.bfloat16
ALU = mybir.AluOpType
AF = mybir.ActivationFunctionType
AX = mybir.AxisListType

D = 1024
SEQ = 4096
CTX = 256
NT = CTX + SEQ
DEPTH = 4
NIN = 3088
DFF = 2816
EPS = 1e-6
GRID = 64
NEG = -30000.0

NDMA_SEM = 8


class Buf:
    __slots__ = ("name", "w", "r", "excl")

    def __init__(self, name, excl=False):
        self.name = name
        self.w = None
        self.r = []
        self.excl = excl


class Op:
    __slots__ = ("eng", "fn", "deps", "sig", "semval", "dma", "dsem", "dval", "dprev")

    def __init__(self, eng, fn, dma):
        self.eng = eng
        self.fn = fn
        self.deps = []
        self.sig = False
        self.semval = 0
        self.dma = dma
        self.dsem = None
        self.dval = 0
        self.dprev = None


class Prog:
    ENGS = ("pe", "act", "dve", "pool", "sp")

    def __init__(self, nc):
        self.nc = nc
        self.ops = {e: [] for e in self.ENGS}
        self.gstack = ExitStack()
        self.stack = self.gstack
        self.ndma = {e: 0 for e in self.ENGS}
        self.dma_last = {}
        self.nbuf = 0
        self.out_dmas = []
        self.bar = {e: [] for e in self.ENGS}
        self.nname = 0

    def sb(self, name, shape, dt):
        self.nname += 1
        return self.stack.enter_context(self.nc.sbuf_tensor(f"{name}_{self.nname}", list(shape), dt))

    def ps(self, name, shape, dt=F32):
        self.nname += 1
        t = self.stack.enter_context(self.nc.psum_tensor(f"{name}_{self.nname}", list(shape), dt))
        return t

    def buf(self, name=None):
        self.nbuf += 1
        return Buf(name or f"b{self.nbuf}")

    def tile(self, name, shape, dt):
        return self.sb(name, shape, dt), self.buf(name)

    def ptile(self, name, shape, dt=F32):
        b = self.buf(name)
        b.excl = True
        return self.ps(name, shape, dt), b

    @contextmanager
    def scope(self):
        old = self.stack
        st = ExitStack()
        self.stack = st
        try:
            yield
        finally:
            self.barrier()
            st.close()
            self.stack = old

    def barrier(self):
        deps = []
        for e in self.ENGS:
            for op in reversed(self.ops[e]):
                if not op.dma:
                    deps.append(op)
                    break
        deps.extend(self.dma_last.values())
        for e in self.ENGS:
            self.bar[e] = list(deps)

    def _add(self, eng, fn, reads, writes, dma=False):
        op = Op(eng, fn, dma)
        deps = []
        ex = [b for b in reads if b.excl]
        if ex:
            reads = [b for b in reads if not b.excl]
            writes = list(writes) + ex
        for b in reads:
            if b.w is not None:
                deps.append((b.w, True))
        for b in writes:
            if b.w is not None:
                deps.append((b.w, b.excl))
            for x in b.r:
                deps.append((x, False))
        if self.bar[eng]:
            for x in self.bar[eng]:
                deps.append((x, True))
            self.bar[eng] = []
        seen = {}
        for d, raw in deps:
            seen[id(d)] = (d, seen.get(id(d), (d, False))[1] or raw)
        for d, raw in seen.values():
            if d is op:
                continue
            if (not d.dma) and d.eng == eng:
                if eng == "pe" or not raw:
                    continue
            op.deps.append(d)
            d.sig = True
        for b in reads:
            b.r.append(op)
        for b in writes:
            b.w = op
            b.r = []
        if dma:
            i = self.ndma[eng]
            self.ndma[eng] += 1
            slot = i % NDMA_SEM
            op.dsem = (eng, slot)
            op.dval = 16 * (i // NDMA_SEM + 1)
            op.dprev = self.dma_last.get((eng, slot))
            self.dma_last[(eng, slot)] = op
        self.ops[eng].append(op)
        return op

    def dma(self, out, in_, r=(), w=(), q="sp", is_out=False, **kw):
        op = self._add(q, lambda e: e.dma_start(out=out, in_=in_, **kw), r, w, dma=True)
        if is_out:
            self.out_dmas.append(op)
        return op

    def mm(self, out, lhsT, rhs, start, stop, r, w):
        return self._add("pe", lambda e: e.matmul(out, lhsT=lhsT, rhs=rhs, start=start, stop=stop), r, w)

    def tr(self, out, in_, ident, r, w):
        return self._add("pe", lambda e: e.transpose(out, in_, ident), r, w)

    def actf(self, out, in_, func, r, w, scale=1.0, bias=0.0):
        return self._add("act", lambda e: e.activation(out=out, in_=in_, func=func, bias=bias, scale=scale), r, w)

    def cp(self, eng, out, in_, r, w):
        if eng == "act":
            return self._add("act", lambda e: e.copy(out=out, in_=in_), r, w)
        return self._add(eng, lambda e: e.tensor_copy(out=out, in_=in_), r, w)

    def tt(self, eng, out, in0, in1, op, r, w):
        return self._add(eng, lambda e: e.tensor_tensor(out=out, in0=in0, in1=in1, op=op), r, w)

    def ts(self, eng, out, in0, s1, s2, op0, op1, r, w):
        if s2 is None:
            return self._add(eng, lambda e: e.tensor_scalar(out=out, in0=in0, scalar1=s1, scalar2=None, op0=op0), r, w)
        return self._add(eng, lambda e: e.tensor_scalar(out=out, in0=in0, scalar1=s1, scalar2=s2, op0=op0, op1=op1), r, w)

    def stt(self, eng, out, in0, scalar, in1, op0, op1, r, w):
        return self._add(
            eng, lambda e: e.scalar_tensor_tensor(out=out, in0=in0, scalar=scalar, in1=in1, op0=op0, op1=op1), r, w
        )

    def recip(self, out, in_, r, w):
        return self._add("dve", lambda e: e.reciprocal(out=out, in_=in_), r, w)

    def memset(self, eng, ap, val, w):
        return self._add(eng, lambda e: e.memset(ap, val), (), w)

    def emit(self):
        nc = self.nc
        st = self.gstack
        sems = {e: st.enter_context(nc.semaphore(f"s_{e}")) for e in self.ENGS}
        dsems = {}
        for e in self.ENGS:
            for k in range(min(NDMA_SEM, self.ndma[e])):
                dsems[(e, k)] = st.enter_context(nc.semaphore(f"d_{e}{k}"))
        for e in self.ENGS:
            c = 0
            for op in self.ops[e]:
                if op.dma:
                    continue
                if op.sig:
                    c += 1
                    op.semval = c
        out_dmas = self.out_dmas
        engmap = {"pe": "tensor", "act": "scalar", "dve": "vector", "pool": "gpsimd", "sp": "sync"}
        stats = {}

        def run(ename, eng):
            known = {}
            nw = 0

            def wait(key, sem, val):
                nonlocal nw
                if known.get(key, 0) >= val:
                    return
                known[key] = val
                eng.wait_ge(sem, val)
                nw += 1

            for op in self.ops[ename]:
                for d in op.deps:
                    if d.dma:
                        wait(d.dsem, dsems[d.dsem], d.dval)
                    else:
                        wait(d.eng, sems[d.eng], d.semval)
                if op.dma and op.dprev is not None:
                    wait(op.dsem, dsems[op.dsem], op.dprev.dval)
                ins = op.fn(eng)
                if op.dma:
                    ins.then_inc(dsems[op.dsem], 16)
                elif op.sig:
                    ins.then_inc(sems[ename], 1)
            if ename == "sp":
                for d in out_dmas:
                    wait(d.dsem, dsems[d.dsem], d.dval)
            stats[ename] = (len(self.ops[ename]), nw)

        with nc.Block() as block:
            for ename in self.ENGS:
                if not self.ops[ename] and ename != "sp":
                    continue
                getattr(block, engmap[ename])(lambda eng, ename=ename: run(ename, eng))
        self.stats = stats
        st.close()
        return nc


def interleave(gens):
    gens = list(gens)
    while gens:
        for g in list(gens):
            try:
                next(g)
            except StopIteration:
                gens.remove(g)


class FreeList:
    def __init__(self, P, name, n, shape, dt, psum=False):
        self.free = []
        for i in range(n):
            t = P.ps(f"{name}{i}", shape, dt) if psum else P.sb(f"{name}{i}", shape, dt)
            b = P.buf(f"{name}{i}")
            b.excl = psum
            self.free.append((t, b))

    def alloc(self):
        assert self.free, "FreeList exhausted"
        return self.free.pop(0)

    def release(self, item):
        self.free.append(item)


def rolling(tasks, width, stagger=2):
    tasks = list(tasks)
    nxt = [0]

    def lane(r):
        for _ in range(r * stagger):
            yield
        while nxt[0] < len(tasks):
            t = tasks[nxt[0]]
            nxt[0] += 1
            yield from t()

    interleave([lane(r) for r in range(width)])


class Rot:
    def __init__(self, P, name, n, shape, dt, psum=False):
        self.items = []
        for i in range(n):
            t = P.ps(f"{name}{i}", shape, dt) if psum else P.sb(f"{name}{i}", shape, dt)
            b = P.buf(f"{name}{i}")
            b.excl = psum
            self.items.append((t, b))
        self.i = 0

    def next(self):
        it = self.items[self.i % len(self.items)]
        self.i += 1
        return it


def host_consts():
    c = {}
    bf = ml_dtypes.bfloat16
    eye = np.eye(128, dtype=np.float32)
    c["ident_f"] = eye
    c["ident_b"] = eye.astype(bf)
    c["ones_b"] = np.ones((128, 128), bf)
    c["ones_f"] = np.ones((128, 128), np.float32)
    bd = np.zeros((128, 128), np.float32)
    bd[:64, :64] = 1
    bd[64:, 64:] = 1
    c["bd_b"] = bd.astype(bf)
    lo = np.zeros((128, 128), np.float32)
    lo[:, :64] = 1
    c["ones_lo"] = lo.astype(bf)
    c["ones_hi"] = (1 - lo).astype(bf)
    ii = np.arange(128)
    tri_f = (ii[:, None] <= ii[None, :]).astype(np.float32)
    c["tri"] = np.stack([tri_f, tri_f.T.copy()])
    c["negm"] = np.stack([np.where(ii[None, :] >= ii[:, None], 0.0, NEG), np.where(ii[None, :] <= ii[:, None], 0.0, NEG)]).astype(np.float32)
    c["offd"] = (1.0 - eye).astype(np.float32)
    t = np.arange(SEQ)
    freqs = (np.float32(10000.0) ** (-(np.arange(32, dtype=np.float32)) / np.float32(32))).astype(np.float32)
    cos_t = np.zeros((128, SEQ), np.float32)
    sin_t = np.zeros((128, SEQ), np.float32)
    rp = np.zeros((128, 128), np.float32)
    for i in range(128):
        pos = (t // GRID if i < 64 else t % GRID).astype(np.float32)
        ang = (pos * freqs[i % 32]).astype(np.float32)
        first = (i % 64) < 32
        cos_t[i] = np.cos(ang)
        sin_t[i] = (-1.0 if first else 1.0) * np.sin(ang)
        rp[i + 32 if first else i - 32, i] = 1.0
    c["rope_cos"] = cos_t
    c["rope_sin"] = sin_t
    c["rperm"] = rp.astype(bf)
    for name, Ls in (("invc_lat", SEQ), ("invc_ctx", CTX)):
        t = np.arange(Ls)
        tab = np.zeros((2, 128, Ls), np.float32)
        for g, win in enumerate((2, 4, 8, 16)):
            lo_ = np.clip(t - win // 2, 0, Ls)
            hi_ = np.clip(t + win - win // 2, 0, Ls)
            tab[g // 2, 64 * (g % 2) : 64 * (g % 2) + 64, :] = (1.0 / (hi_ - lo_).astype(np.float32))[None, :]
        c[name] = tab
    return c


NA_CLASSES = {0: 1, 1: 2, 30: 3, 31: 4}


def na_chunks(R):
    rows = set()
    for r in (2 * R, 2 * R + 1):
        r0 = min(max(r - 4, 0), GRID - 8)
        rows.update(range(r0, r0 + 8))
    return sorted({kr // 2 for kr in rows})


def na_bias_sets(rpb):
    H = rpb.shape[0]
    out = np.full((5, 128, H, 5, 128), NEG, np.float32)
    kc = np.arange(64)[:, None]
    qc = np.arange(64)[None, :]
    c0 = np.clip(qc - 8, 0, GRID - 16)
    cvis = (kc >= c0) & (kc < c0 + 16)
    dc = np.clip(kc - qc + 15, 0, 30)
    for cls, R in ((0, 10), (1, 0), (2, 1), (3, 30), (4, 31)):
        for slot, m in enumerate(na_chunks(R)):
            for kj in range(2):
                kr = 2 * m + kj
                for qi in range(2):
                    r = 2 * R + qi
                    r0 = min(max(r - 4, 0), GRID - 8)
                    if not (r0 <= kr < r0 + 8):
                        continue
                    dr = kr - r + 7
                    blk = rpb[:, dr][:, dc]
                    blk = np.where(cvis[None], blk, np.float32(NEG))
                    out[cls, 64 * kj : 64 * kj + 64, :, slot, 64 * qi : 64 * qi + 64] = blk.transpose(1, 0, 2)
    return out


CONST_SPECS = {
    "ident_f": ([128, 128], F32),
    "ident_b": ([128, 128], BF16),
    "ones_b": ([128, 128], BF16),
    "ones_f": ([128, 128], F32),
    "bd_b": ([128, 128], BF16),
    "ones_lo": ([128, 128], BF16),
    "ones_hi": ([128, 128], BF16),
    "offd": ([128, 128], F32),
    "rperm": ([128, 128], BF16),
}
BIG_CONSTS = {
    "invc_lat": ([2, 128, SEQ], F32),
    "invc_ctx": ([2, 128, CTX], F32),
    "na_bias": ([DEPTH, 5, 128, 4 * 5 * 128], F32),
    "tri": ([2, 128, 128], F32),
    "negm": ([2, 128, 128], F32),
    "rope_cos": ([128, SEQ], F32),
    "rope_sin": ([128, SEQ], F32),
}

W_SPECS = {
    "w_ada": [DEPTH, D, 6 * D],
    "b_ada": [DEPTH, 6 * D],
    "norm_mix": [DEPTH, D],
    "w_in": [DEPTH, D, NIN],
    "pool_w": [DEPTH, 4, 64, 64],
    "pool_scale": [DEPTH, 256],
    "gdn_conv": [DEPTH, 5, 1536],
    "gdn_a_log": [DEPTH, 2, 4],
    "gdn_dt_bias": [DEPTH, 2, 4],
    "gdn_norm": [DEPTH, 128],
    "na_q_norm": [DEPTH, 64],
    "na_k_norm": [DEPTH, 64],
    "na_rpb": [DEPTH, 4, 15, 31],
    "w_out": [DEPTH, D, D],
    "norm_ffn": [DEPTH, D],
    "w_up": [DEPTH, D, 2 * DFF],
    "ffn_conv": [DEPTH, 3, 2 * DFF],
    "w_down": [DEPTH, DFF, D],
}

F_COLS = [0, 128] + [256 + 128 * i for i in range(12)] + [2320 + 128 * i for i in range(4)]
NF = len(F_COLS)


def fm(ap3):
    return ap3.rearrange("k p t -> p k t")


class Kern:
    def __init__(self, cfg):
        self.cfg = cfg
        nc = bass.Bass("TRN2", target_bir_lowering=False)
        self.nc = nc
        self.P = Prog(nc)
        ext = cfg.get("ext", {})

        def dram(name, shape, dt, kind="Internal"):
            kind = ext.get(name, kind)
            return nc.dram_tensor(name, list(shape), dt, kind=kind).ap()

        self.x = dram("x", [SEQ, D], F32, "ExternalInput")
        self.ctx = dram("ctx", [CTX, D], F32, "ExternalInput")
        self.c = dram("c", [D], F32, "ExternalInput")
        self.c_ctx = dram("c_ctx", [D], F32, "ExternalInput")
        self.w = {k: dram(k, s, F32, "ExternalInput") for k, s in W_SPECS.items()}
        self.cst = {k: dram(k, s, dt, "ExternalInput") for k, (s, dt) in CONST_SPECS.items()}
        self.big = {k: dram(k, s, dt, "ExternalInput") for k, (s, dt) in BIG_CONSTS.items()}
        self.out = dram("out", [SEQ, D], F32, "ExternalOutput")
        self.resT = dram("resT", [8, 128, NT], F32)
        self.pF = dram("pF", [NF, 128, NT], BF16)
        self.pT = dram("pT", [NT, 768], BF16)
        self.yT = dram("yT", [8, 128, NT], BF16)
        self.hfT = dram("hfT", [8, 128, NT], BF16)
        self.gQT = dram("gQT", [4, 128, NT], BF16)
        self.gKT = dram("gKT", [4, 128, NT], BF16)
        self.gK = dram("gK", [NT, 512], BF16)
        self.gV = dram("gV", [NT, 512], BF16)
        self.gO = [dram("gOf", [NT, 512], F32), dram("gOb", [NT, 512], F32)]

    def load_consts(self):
        P = self.P
        self.k = {}
        for name, (shape, dt) in CONST_SPECS.items():
            t, b = P.tile("c_" + name, shape, dt)
            P.dma(t[:], self.cst[name][:, :], w=[b])
            self.k[name] = (t, b)
        self.mods = []
        for par in range(2):
            self.mods.append(dict(modv=P.tile(f"modv{par}", [128, 48, 2], F32), A1=P.tile(f"A1{par}", [128, 8, 2], F32),
                                  A2=P.tile(f"A2{par}", [128, 8, 2], F32)))
        self.set_layer(0)
        self.scT, self.bsc = P.tile("scT", [128, 8, 2], F32)
        with P.scope():
            self.lc_setup()
            self.load_cols(self.scT[:, :, 0], self.bsc, self.c.rearrange("(k p) -> k p", p=128), 8)
            self.load_cols(self.scT[:, :, 1], self.bsc, self.c_ctx.rearrange("(k p) -> k p", p=128), 8)
        P.actf(self.scT[:], self.scT[:], AF.Silu, [self.bsc], [self.bsc])

    def set_layer(self, l):
        m = self.mods[l % 2]
        self.modv, self.bmod = m["modv"]
        self.A1, self.bA1 = m["A1"]
        self.A2, self.bA2 = m["A2"]

    def lc_setup(self):
        self.lc_in = Rot(self.P, "lcin", 2, [128, 128], F32)
        self.lc_ps = self.P.ptile("lcps", [128, 128])

    def load_cols(self, dst, bdst, src2d, n):
        P = self.P
        ident, bi = self.k["ident_f"]
        ti, bti = self.lc_in.next()
        P.dma(ti[:n, :], src2d, w=[bti])
        pt, bp = self.lc_ps
        P.tr(pt[:, :n], ti[:n, :], ident[:n, :n], [bti, bi], [bp])
        P.cp("dve", dst, pt[:, :n], [bp], [bdst])

    def load_w_bf16(self, dst, src, K, N, stg, CH, order="col"):
        P = self.P
        bufs = {}
        nch = (N + CH - 1) // CH
        idx = [(k, c) for c in range(nch) for k in range(K)] if order == "col" else [(k, c) for k in range(K) for c in range(nch)]
        engs = ("pool", "act", "dve")
        for i, (k, c) in enumerate(idx):
            c0, c1 = c * CH, min(N, (c + 1) * CH)
            st, bs = stg.next()
            b = P.buf(f"w{k}_{c}")
            bufs[(k, c)] = b
            P.dma(st[:, : c1 - c0], src[k * 128 : (k + 1) * 128, c0:c1], w=[bs])
            P.cp(engs[i % 3], dst[:, k, c0:c1], st[:, : c1 - c0], [bs], [b])
        return bufs

    def phase_init(self):
        P = self.P
        ident, bi = self.k["ident_f"]
        with P.scope():
            xin = Rot(P, "xin", 2, [128, D], F32)
            pst = Rot(P, "pst", 2, [128, 8, 128], F32, psum=True)
            xo = Rot(P, "xo", 2, [128, 8, 128], F32)
            for s in range(NT // 128):
                t0 = s * 128
                src = self.ctx[t0 : t0 + 128, :] if t0 < CTX else self.x[t0 - CTX : t0 - CTX + 128, :]
                xt, bx = xin.next()
                P.dma(xt[:], src, w=[bx])
                pt, bp = pst.next()
                for k in range(8):
                    P.tr(pt[:, k, :], xt[:, k * 128 : (k + 1) * 128], ident[:], [bx, bi], [bp])
                ot, bo = xo.next()
                P.cp("act" if s % 2 else "dve", ot[:], pt[:], [bp], [bo])
                P.dma(fm(self.resT)[:, :, t0 : t0 + 128], ot[:], r=[bo], q="pool")

    def phase_ada(self, l):
        with self.P.scope():
            for _ in self.ada_gen(l):
                pass

    def ada_gen(self, l):
        P = self.P
        w_ada = self.w["w_ada"][l]
        m = self.mods[l % 2]
        modv, bmod = m["modv"]
        self.lc_setup()
        stg = Rot(P, "adast", 2, [128, 8, 1024], F32)
        pm, bpm = P.ptile("pmod", [128, 48, 2])
        bt, bb = P.tile("bada", [128, 48], F32)
        nt_, bn = P.tile("nrm", [128, 8, 2], F32)
        self.load_cols(bt[:], bb, self.w["b_ada"][l].rearrange("(j p) -> j p", p=128), 48)
        self.load_cols(nt_[:, :, 0], bn, self.w["norm_mix"][l].rearrange("(k p) -> k p", p=128), 8)
        self.load_cols(nt_[:, :, 1], bn, self.w["norm_ffn"][l].rearrange("(k p) -> k p", p=128), 8)
        for g in range(6):
            st, bs = stg.next()
            P.dma(st[:], w_ada[:, g * 1024 : (g + 1) * 1024].rearrange("(k p) n -> p k n", p=128), w=[bs])
            for n in range(8):
                for k in range(8):
                    P.mm(pm[:, g * 8 + n, :], st[:, k, n * 128 : (n + 1) * 128], self.scT[:, k, :],
                         k == 0, k == 7, [bs, self.bsc], [bpm])
                if n % 4 == 3:
                    yield
        P.tt("dve", modv[:], pm[:], bt[:].unsqueeze(2).to_broadcast([128, 48, 2]), ALU.add, [bpm, bb], [bmod])
        for (key, sidx, nidx) in (("A1", 1, 0), ("A2", 4, 1)):
            A, bA = m[key]
            P.ts("dve", A[:], modv[:, sidx * 8 : sidx * 8 + 8, :], 1.0, 32.0, ALU.add, ALU.mult, [bmod], [bA])
            P.tt("dve", A[:], A[:], nt_[:, :, nidx : nidx + 1].to_broadcast([128, 8, 2]), ALU.mult, [bA, bn], [bA])

    def mod(self, j, k, col):
        return self.modv[:, j * 8 + k, col : col + 1]

    def mk_norm_tiles(self):
        P = self.P
        nt = {}
        nt["sq"] = P.tile("sq", [128, 8, 512], BF16)
        nt["ss"] = P.ptile("ss", [128, 512])
        nt["rs"] = P.tile("rs", [128, 512], F32)
        nt["t"] = P.tile("nt", [128, 8, 512], F32)
        return nt

    def norm_mod(self, nt, xt, bx, W, A, bA, shj, col, hT, bh):
        P = self.P
        ones, bo = self.k["ones_b"]
        sq, bsq = nt["sq"]
        ss, bss = nt["ss"]
        rs, brs = nt["rs"]
        t, bt = nt["t"]
        P.actf(sq[:, :, :W], xt[:, :, :W], AF.Square, [bx], [bsq])
        for k in range(8):
            P.mm(ss[:, :W], ones[:], sq[:, k, :W], k == 0, k == 7, [bo, bsq], [bss])
        P.actf(rs[:, :W], ss[:, :W], AF.Sqrt, [bss], [brs], scale=1.0, bias=1024.0 * EPS)
        P.recip(rs[:, :W], rs[:, :W], [brs], [brs])
        P.tt("dve", t[:, :, :W], xt[:, :, :W], rs[:, :W].unsqueeze(1).to_broadcast([128, 8, W]), ALU.mult,
             [bx, brs], [bt])
        for k in range(8):
            P.actf(hT[:, k, :W], t[:, k, :W], AF.Identity, [bt, bA, self.bmod], [bh],
                   scale=A[:, k, col : col + 1], bias=self.mod(shj, k, col))

    def tiles_A(self):
        tl = [(0, CTX, 1)]
        for j in range(SEQ // 512):
            tl.append((CTX + 512 * j, 512, 0))
        return tl

    def phase_A(self, l):
        P = self.P
        with P.scope():
            Win, _ = P.tile("Win", [128, 8, NIN], BF16)
            stg = Rot(P, "wst", 3, [128, 772], F32)
            nt = self.mk_norm_tiles()
            xin = Rot(P, "xa", 2, [128, 8, 512], F32)
            hts = Rot(P, "hT", 2, [128, 8, 512], BF16)
            psF = Rot(P, "psF", 3, [128, 512], F32, psum=True)
            psT1 = Rot(P, "psT1", 1, [128, 512], F32, psum=True)
            psT2 = Rot(P, "psT2", 1, [128, 272], F32, psum=True)
            ob = Rot(P, "obF", 2, [128, 6, 512], BF16)
            tb = Rot(P, "obT", 2, [128, 768], BF16)
            ei = 0
            wb = None

            def bW(c_lo, c_hi):
                return [wb[(k, c)] for k in range(8) for c in range(c_lo // 772, (c_hi - 1) // 772 + 1)]

            for (c0, W, col) in self.tiles_A():
                xt, bx = xin.next()
                P.dma(xt[:, :, :W], fm(self.resT)[:, :, c0 : c0 + W], w=[bx])
                if wb is None:
                    wb = self.load_w_bf16(Win, self.w["w_in"][l], 8, NIN, stg, 772)
                hT, bh = hts.next()
                self.norm_mod(nt, xt, bx, W, self.A1, self.bA1, 0, col, hT, bh)
                for g in range(3):
                    o, bo = ob.next()
                    for j in range(6):
                        ch = g * 6 + j
                        ps, bp = psF.next()
                        for k in range(8):
                            P.mm(ps[:, :W], Win[:, k, F_COLS[ch] : F_COLS[ch] + 128], hT[:, k, :W],
                                 k == 0, k == 7, [wb[(k, F_COLS[ch] // 772)], wb[(k, (F_COLS[ch] + 127) // 772)], bh], [bp])
                        P.cp("act" if ei % 2 else "dve", o[:, j, :W], ps[:, :W], [bp], [bo])
                        ei += 1
                    P.dma(fm(self.pF)[:, g * 6 : g * 6 + 6, c0 : c0 + W], o[:, :, :W], r=[bo], q="pool")
                for sbk in range(W // 128):
                    t0 = c0 + sbk * 128
                    p1, b1 = psT1.next()
                    p2, b2 = psT2.next()
                    for k in range(8):
                        P.mm(p1[:], hT[:, k, sbk * 128 : (sbk + 1) * 128], Win[:, k, 1792:2304],
                             k == 0, k == 7, [wb[(k, 2)], bh], [b1])
                    for k in range(8):
                        P.mm(p2[:, 0:256], hT[:, k, sbk * 128 : (sbk + 1) * 128], Win[:, k, 2832:3088],
                             k == 0, k == 7, [wb[(k, 3)], bh], [b2])
                    for k in range(8):
                        P.mm(p2[:, 256:272], hT[:, k, sbk * 128 : (sbk + 1) * 128], Win[:, k, 2304:2320],
                             k == 0, k == 7, [wb[(k, 2)], wb[(k, 3)], bh], [b2])
                    o, bo = tb.next()
                    P.cp("act", o[:, 0:512], p1[:], [b1], [bo])
                    P.cp("dve", o[:, 512:768], p2[:, 0:256], [b2], [bo])
                    P.cp("dve", self.gab[:, t0 // 128, :], p2[:, 256:272], [b2], [self.bgab])
                    P.dma(self.pT[t0 : t0 + 128, :], o[:], r=[bo], q="pool")

    def phase_C1(self, l, last):
        P = self.P
        with P.scope():
            Wo, _ = P.tile("Wout", [128, 8, D], BF16)
            stg = Rot(P, "wst", 3, [128, 512], F32)
            wb = None
            nt = self.mk_norm_tiles()
            xin = Rot(P, "xc", 2, [128, 8, 512], F32)
            yin = Rot(P, "yc", 2, [128, 8, 512], BF16)
            xms = Rot(P, "xm", 2, [128, 8, 512], F32)
            hts = Rot(P, "hfo", 2, [128, 8, 512], BF16)
            pso = Rot(P, "pso", 3, [128, 512], F32, psum=True)
            for (c0, W, col) in self.tiles_A():
                if last and col == 1:
                    continue
                xt, bx = xin.next()
                yt, by = yin.next()
                P.dma(xt[:, :, :W], fm(self.resT)[:, :, c0 : c0 + W], w=[bx])
                P.dma(yt[:, :, :W], fm(self.yT)[:, :, c0 : c0 + W], w=[by])
                if wb is None:
                    wb = self.load_w_bf16(Wo, self.w["w_out"][l], 8, D, stg, 512)
                xm, bxm = xms.next()
                for n in range(8):
                    ps, bp = pso.next()
                    for k in range(8):
                        P.mm(ps[:, :W], Wo[:, k, n * 128 : (n + 1) * 128], yt[:, k, :W], k == 0, k == 7,
                             [wb[(k, n // 4)], by], [bp])
                    P.stt("dve", xm[:, n, :W], ps[:, :W], self.mod(2, n, col), xt[:, n, :W], ALU.mult, ALU.add,
                          [bp, bx, self.bmod], [bxm])
                P.dma(fm(self.resT)[:, :, c0 : c0 + W], xm[:, :, :W], r=[bxm], q="pool")
                hT, bh = hts.next()
                self.norm_mod(nt, xm, bxm, W, self.A2, self.bA2, 3, col, hT, bh)
                P.dma(fm(self.hfT)[:, :, c0 : c0 + W], hT[:, :, :W], r=[bh], q="pool")

    def tiles_C2(self, last):
        tl = []
        if not last:
            tl.append((0, CTX, 0, CTX, 1))
        step = 456
        s = 0
        while s < SEQ:
            e = min(SEQ, s + step)
            tl.append((CTX, SEQ, s, e, 0))
            s = e
        return tl

    def phase_C2(self, l, last):
        P = self.P
        with P.scope():
            Wu, _ = P.tile("Wup", [128, 8, 2 * DFF], BF16)
            Wd, _ = P.tile("Wdn", [128, 22, D], BF16)
            stg = Rot(P, "wst", 3, [128, 1024], F32)
            wbu = wbd = None
            self.lc_setup()
            cw, bcw = P.tile("cw", [128, 3, 44], F32)
            for j in range(3):
                self.load_cols(cw[:, j, :], bcw, self.w["ffn_conv"][l][j].rearrange("(c p) -> c p", p=128), 44)
            hin = Rot(P, "hin", 2, [128, 8, 458], BF16)
            xin = Rot(P, "xf", 2, [128, 456], F32)
            xos = Rot(P, "xo2", 2, [128, 456], F32)
            gts = Rot(P, "gT", 1, [128, 22, 456], BF16)
            psa = Rot(P, "psa", 2, [128, 458], F32, psum=True)
            psb = Rot(P, "psb", 2, [128, 458], F32, psum=True)
            psd = Rot(P, "psd", 2, [128, 456], F32, psum=True)
            cas = Rot(P, "ca", 2, [128, 456], F32)
            cbs = Rot(P, "cb", 2, [128, 456], F32)
            sas = Rot(P, "sa", 2, [128, 456], F32)
            for (so, sl, s, e, col) in self.tiles_C2(last):
                Wo = e - s
                W = Wo + 2
                lo, hi = s - 1, e + 1
                ht, bh = hin.next()
                a0 = 1 if lo < 0 else 0
                a1 = W - 1 if hi > sl else W
                P.dma(ht[:, :, a0:a1], fm(self.hfT)[:, :, so + lo + a0 : so + lo + a1], w=[bh])
                if a0:
                    P.memset("pool", ht[:, :, 0:1], 0.0, [bh])
                if a1 < W:
                    P.memset("pool", ht[:, :, W - 1 : W], 0.0, [bh])
                if wbu is None:
                    wbu = {}
                    for blk in range(0, DFF, 704):
                        for half in range(2):
                            c0w = half * DFF + blk
                            sub = self.load_w_bf16(Wu[:, :, c0w : c0w + 704], self.w["w_up"][l][:, c0w : c0w + 704], 8, 704, stg, 704)
                            for (k, _c), b_ in sub.items():
                                wbu[(k, c0w // 704)] = b_
                    wbd = self.load_w_bf16(Wd, self.w["w_down"][l], 22, D, stg, 1024, order="row")
                gT, bg = gts.next()
                for f in range(22):
                    halves = []
                    for (half, psr, cs) in ((0, psa, cas), (1, psb, cbs)):
                        ch = half * 22 + f
                        ps, bp = psr.next()
                        for k in range(8):
                            P.mm(ps[:, :W], Wu[:, k, ch * 128 : (ch + 1) * 128], ht[:, k, :W], k == 0, k == 7,
                                 [wbu[(k, (ch * 128) // 704)], wbu[(k, (ch * 128 + 127) // 704)], bh], [bp])
                        ct, bc = cs.next()
                        P.actf(ct[:, :Wo], ps[:, 1 : W - 1], AF.Identity, [bp, bcw], [bc], scale=cw[:, 1, ch : ch + 1])
                        P.stt("dve", ct[:, :Wo], ps[:, 0 : W - 2], cw[:, 0, ch : ch + 1], ct[:, :Wo], ALU.mult, ALU.add,
                              [bp, bcw, bc], [bc])
                        P.stt("dve", ct[:, :Wo], ps[:, 2:W], cw[:, 2, ch : ch + 1], ct[:, :Wo], ALU.mult, ALU.add,
                              [bp, bcw, bc], [bc])
                        halves.append((ct, bc))
                    (ca, bca), (cb, bcb) = halves
                    sa, bsa = sas.next()
                    P.actf(sa[:, :Wo], ca[:, :Wo], AF.Silu, [bca], [bsa])
                    P.tt("pool", gT[:, f, :Wo], sa[:, :Wo], cb[:, :Wo], ALU.mult, [bsa, bcb], [bg])
                for n in range(8):
                    xt, bx = xin.next()
                    P.dma(xt[:, :Wo], self.resT[n, :, so + s : so + e], w=[bx])
                    ps, bp = psd.next()
                    for f in range(22):
                        P.mm(ps[:, :Wo], Wd[:, f, n * 128 : (n + 1) * 128], gT[:, f, :Wo], f == 0, f == 21,
                             [wbd[(f, 0)], bg], [bp])
                    xo, bxo = xos.next()
                    P.stt("dve", xo[:, :Wo], ps[:, :Wo], self.mod(5, n, col), xt[:, :Wo], ALU.mult, ALU.add,
                          [bp, bx, self.bmod], [bxo])
                    P.dma(self.resT[n, :, so + s : so + e], xo[:, :Wo], r=[bxo], q="pool")

    def phase_final(self):
        P = self.P
        ident, bi = self.k["ident_f"]
        with P.scope():
            xin = Rot(P, "fin", 2, [128, 8, 128], F32)
            pst = Rot(P, "psf", 2, [128, 1024], F32, psum=True)
            xo = Rot(P, "fout", 2, [128, 1024], F32)
            for s in range(SEQ // 128):
                t0 = CTX + s * 128
                xt, bx = xin.next()
                P.dma(xt[:], fm(self.resT)[:, :, t0 : t0 + 128], w=[bx])
                pt, bp = pst.next()
                for k in range(8):
                    P.tr(pt[:, k * 128 : (k + 1) * 128], xt[:, k, :], ident[:], [bx, bi], [bp])
                ot, bo = xo.next()
                P.cp("act" if s % 2 else "dve", ot[:], pt[:], [bp], [bo])
                P.dma(self.out[s * 128 : (s + 1) * 128, :], ot[:], r=[bo], q="pool", is_out=True)

    def build(self):
        P = self.P
        cfg = self.cfg
        self.ada_done = set()
        self.lc_in = None
        self.load_consts()
        self.gab, self.bgab = P.tile("gab", [128, NT // 128, 16], F32)
        layers = cfg.get("layers", list(range(DEPTH)))
        phases = cfg.get("phases", "0aABCF")
        if "0" in phases:
            self.phase_init()
        for l in layers:
            last = (l == DEPTH - 1) or cfg.get("force_last", False)
            self.set_layer(l)
            if "a" in phases and l not in self.ada_done:
                self.phase_ada(l)
            if "A" in phases:
                self.phase_A(l)
            if "B" in phases:
                self.phase_B(l, last)
            if "C" in phases:
                self.phase_C1(l, last)
                self.phase_C2(l, last)
        if "F" in phases:
            self.phase_final()
        P.emit()
        return self.nc

    def phase_B(self, l, last):
        sub = self.cfg.get("mixers", "png")
        if "p" in sub:
            self.phase_pool(l, last)
        if "n" in sub:
            self.phase_na(l, last)
        if "g" in sub:
            self.phase_gdn(l, last)

    def phase_pool(self, l, last):
        P = self.P
        with P.scope():
            gens = [self.pool_gen(l, last)]
            if self.cfg.get("ada_overlap", True) and (l + 1) in self.cfg.get("layers", list(range(DEPTH))):
                gens.append(self.ada_gen(l + 1))
                self.ada_done.add(l + 1)
            interleave(gens)

    def pool_gen(self, l, last):
        P = self.P
        if True:
            pwf, bpf = P.tile("pwf", [128, 2, 128], F32)
            pw, bpw = P.tile("pw", [128, 2, 128], BF16)
            P.memset("pool", pwf[:], 0.0, [bpf])
            for g in range(4):
                h = 64 * (g % 2)
                P.dma(pwf[h : h + 64, g // 2, h : h + 64], self.w["pool_w"][l][g], w=[bpf])
            P.cp("pool", pw[:], pwf[:], [bpf], [bpw])
            self.lc_setup()
            sc, bsc = P.tile("psc", [128, 2], F32)
            self.load_cols(sc[:], bsc, self.w["pool_scale"][l].rearrange("(c p) -> c p", p=128), 2)
            xbs = Rot(P, "pxb", 2, [128, 2, 528], BF16)
            ics = Rot(P, "pic", 2, [128, 2, 512], F32)
            lv = [P.tile(f"plv{i}", [128, 2, 528], F32) for i in range(4)]
            mt, bm = P.tile("pm", [128, 2, 512], F32)
            pls = Rot(P, "ppl", 2, [128, 2, 512], BF16)
            pso = Rot(P, "pps", 2, [128, 512], F32, psum=True)
            yas = Rot(P, "pya", 2, [128, 2, 512], BF16)
            seqs = [(CTX, SEQ, "invc_lat")]
            if not last:
                seqs.append((0, CTX, "invc_ctx"))
            for (so, sl, icn) in seqs:
                for s0 in range(0, sl, 512):
                    Wo = min(512, sl - s0)
                    W = Wo + 16
                    xb, bx = xbs.next()
                    a0 = max(0, 8 - s0)
                    a1 = min(W, sl - s0 + 8)
                    P.dma(xb[:, :, a0:a1], fm(self.pF)[:, 0:2, so + s0 - 8 + a0 : so + s0 - 8 + a1], w=[bx])
                    if a0 > 0:
                        P.memset("pool", xb[:, :, 0:a0], 0.0, [bx])
                    if a1 < W:
                        P.memset("pool", xb[:, :, a1:W], 0.0, [bx])
                    ic, bic = ics.next()
                    P.dma(ic[:, :, :Wo], fm(self.big[icn])[:, :, s0 : s0 + Wo], w=[bic])
                    (s2, b2), (s4, b4), (s8, b8), (s16, b16) = lv
                    P.tt("pool", s2[:, :, 1:W], xb[:, :, 1:W], xb[:, :, 0 : W - 1], ALU.add, [bx], [b2])
                    P.tt("pool", s4[:, :, 3:W], s2[:, :, 3:W], s2[:, :, 1 : W - 2], ALU.add, [b2], [b4])
                    P.tt("pool", s8[:, :, 7:W], s4[:, :, 7:W], s4[:, :, 3 : W - 4], ALU.add, [b4], [b8])
                    P.tt("pool", s16[:, :, 15:W], s8[:, :, 15:W], s8[:, :, 7 : W - 8], ALU.add, [b8], [b16])
                    for (c, h, src, bs_, sh) in ((0, 0, s2, b2, 0), (0, 1, s4, b4, 1), (1, 0, s8, b8, 3), (1, 1, s16, b16, 7)):
                        pr = slice(64 * h, 64 * h + 64)
                        P.tt("pool", mt[pr, c, :Wo], src[pr, c, 8 + sh : 8 + sh + Wo], ic[pr, c, :Wo], ALU.mult,
                             [bs_, bic], [bm])
                    pl, bpl = pls.next()
                    P.tt("pool", pl[:, :, :Wo], mt[:, :, :Wo], xb[:, :, 8 : 8 + Wo], ALU.subtract, [bm, bx], [bpl])
                    ya, bya = yas.next()
                    for c in range(2):
                        ps, bp = pso.next()
                        P.mm(ps[:, :Wo], pw[:, c, :], pl[:, c, :Wo], True, True, [bpw, bpl], [bp])
                        P.ts("dve", ya[:, c, :Wo], ps[:, :Wo], sc[:, c : c + 1], None, ALU.mult, None, [bp, bsc], [bya])
                    P.dma(fm(self.yT)[:, 0:2, so + s0 : so + s0 + Wo], ya[:, :, :Wo], r=[bya], q="act")
                    yield

    def phase_na(self, l, last):
        P = self.P
        bd, bbd = self.k["bd_b"]
        with P.scope():
            qk, bqk = P.tile("naqk", [128, 4, NT], BF16)
            vp, bvp = P.tile("navp", [128, NT // 128, 4, 128], BF16)
            gn, bgn = P.tile("nagn", [128, 2], F32)
            with P.scope():
                self.lc_setup()
                ti, bti = self.lc_in.next()
                for j, nm in enumerate(("na_q_norm", "na_k_norm")):
                    for hh in range(2):
                        P.dma(ti[j : j + 1, 64 * hh : 64 * hh + 64], self.w[nm][l].rearrange("(o d) -> o d", o=1), w=[bti])
                pt, bp = self.lc_ps
                ident, bi = self.k["ident_f"]
                P.tr(pt[:, :2], ti[:2, :], ident[:2, :2], [bti, bi], [bp])
                P.cp("dve", gn[:], pt[:, :2], [bp], [bgn])
                P.ts("dve", gn[:, 1:2], gn[:, 1:2], 8.0, None, ALU.mult, None, [bgn], [bgn])
            P.memset("pool", vp[:], 0.0, [bvp])
            for h in range(4):
                o = 64 * (h % 2)
                P.dma(vp[:, :, h, o : o + 64],
                      self.pT[:, 512 + 64 * h : 512 + 64 * h + 64].rearrange("(t p) c -> p t c", p=128), w=[bvp])
            with P.scope():
                xbs = Rot(P, "nxb", 2, [128, 512], BF16)
                sqs = Rot(P, "nsq", 2, [128, 512], BF16)
                rss = Rot(P, "nrs", 2, [128, 512], F32)
                pss = Rot(P, "nps", 2, [128, 512], F32, psum=True)
                for c in range(4):
                    for t0 in range(0, NT, 512):
                        W = min(512, NT - t0)
                        xb, bx = xbs.next()
                        P.dma(xb[:, :W], self.pF[14 + c, :, t0 : t0 + W], w=[bx])
                        sq, bsq = sqs.next()
                        P.actf(sq[:, :W], xb[:, :W], AF.Square, [bx], [bsq])
                        ps, bps = pss.next()
                        P.mm(ps[:, :W], bd[:], sq[:, :W], True, True, [bbd, bsq], [bps])
                        rs, brs = rss.next()
                        P.actf(rs[:, :W], ps[:, :W], AF.Sqrt, [bps], [brs], scale=1.0, bias=64.0 * EPS)
                        P.recip(rs[:, :W], rs[:, :W], [brs], [brs])
                        P.stt("dve", qk[:, c, t0 : t0 + W], xb[:, :W], gn[:, c // 2 : c // 2 + 1], rs[:, :W],
                              ALU.mult, ALU.mult, [bx, bgn, brs], [bqk])
            olo, bol = self.k["ones_lo"]
            ohi, boh = self.k["ones_hi"]
            bis = Rot(P, "nab", 2, [128, 4, 5, 128], F32)
            pS = Rot(P, "naS", 2, [128, 7, 128], F32, psum=True)
            pO = Rot(P, "naO", 2, [128, 512], F32, psum=True)
            pD = Rot(P, "naD", 2, [128, 512], F32, psum=True)
            sbs = Rot(P, "nas", 4, [128, 5, 128], F32)
            pts = Rot(P, "nap", 6, [128, 7, 128], BF16)
            rds = Rot(P, "nar", 4, [128, 128], F32)
            ycs = Rot(P, "nay", 4, [128, 2, 128], BF16)
            groups = []
            for R in range(32):
                ch = na_chunks(R)
                groups.append((CTX + 128 * R, [(CTX + 128 * m, 2 + m) for m in ch], NA_CLASSES.get(R, 0)))
            if not last:
                for Rc in range(2):
                    groups.append((128 * Rc, [], None))
            cur = {"cls": None, "bt": None, "bbt": None}

            def group(q0, band, cls):
                if cls is not None and cls != cur["cls"]:
                    cur["bt"], cur["bbt"] = bis.next()
                    P.dma(cur["bt"][:], self.big["na_bias"][l, cls].rearrange("p (h s q) -> p h s q", h=4, s=5), w=[cur["bbt"]])
                    cur["cls"] = cls
                bt, bbt = cur["bt"], cur["bbt"]
                ns = len(band)
                keys = band + [(0, 0), (128, 1)]
                slots = list(range(ns)) + [5, 6]
                yc, byc = ycs.next()
                for c in range(2):
                    pps = []
                    for hh in range(2):
                        h = 2 * c + hh
                        pr = slice(64 * hh, 64 * hh + 64)
                        ps, bps = pS.next()
                        for sl_, (k0, vt) in zip(slots, keys):
                            P.mm(ps[:, sl_, :], qk[pr, 2 + c, k0 : k0 + 128], qk[pr, c, q0 : q0 + 128], True, True,
                                 [bqk], [bps])
                        yield
                        pp, bpp = pts.next()
                        if ns:
                            sb_, bsb = sbs.next()
                            P.tt("dve", sb_[:, :ns, :], ps[:, :ns, :], bt[:, h, :ns, :], ALU.add, [bps, bbt], [bsb])
                            P.actf(pp[:, :ns, :], sb_[:, :ns, :], AF.Exp, [bsb], [bpp])
                        P.actf(pp[:, 5:7, :], ps[:, 5:7, :], AF.Exp, [bps], [bpp])
                        pps.append((pp, bpp))
                        yield
                    po, bpo = pO.next()
                    pd, bpd = pD.next()
                    nmm = 2 * len(keys)
                    imm = 0
                    for hh in range(2):
                        h = 2 * c + hh
                        pp, bpp = pps[hh]
                        for sl_, (k0, vt) in zip(slots, keys):
                            P.mm(po[:, 0:128], vp[:, vt, h, :], pp[:, sl_, :], imm == 0, imm == nmm - 1, [bvp, bpp], [bpo])
                            imm += 1
                    imm = 0
                    for hh in range(2):
                        pp, bpp = pps[hh]
                        onesp, bon = (olo, bol) if hh == 0 else (ohi, boh)
                        for sl_, (k0, vt) in zip(slots, keys):
                            P.mm(pd[:, 0:128], onesp[:], pp[:, sl_, :], imm == 0, imm == nmm - 1, [bon, bpp], [bpd])
                            imm += 1
                    yield
                    rd, brd = rds.next()
                    P.recip(rd[:], pd[:, 0:128], [bpd], [brd])
                    P.tt("dve", yc[:, c, :], po[:, 0:128], rd[:], ALU.mult, [bpo, brd], [byc])
                    yield
                P.dma(fm(self.yT)[:, 6:8, q0 : q0 + 128], yc[:], r=[byc], q="pool")

            for i in range(0, len(groups), 2):
                interleave([group(*g) for g in groups[i : i + 2]])

    def phase_gdn(self, l, last):
        P = self.P
        ident_f, bif = self.k["ident_f"]
        ident_b, bib = self.k["ident_b"]
        ones_b, bob = self.k["ones_b"]
        ones_f, bof = self.k["ones_f"]
        offd, bofd = self.k["offd"]
        rperm, brp = self.k["rperm"]
        NTL = NT // 128
        with P.scope():
            gcol, bg = P.tile("g_col", [128, NTL, 8], F32)
            beta, bbe = P.tile("g_beta", [128, NTL, 8], F32)
            nbeta, bnb = P.tile("g_nbeta", [128, NTL, 8], F32)
            cst8, bc8 = P.tile("g_c8", [128, 2, 8], F32)
            P.dma(cst8[:, 0, :], self.w["gdn_dt_bias"][l].rearrange("d h -> (d h)").partition_broadcast(128), w=[bc8])
            P.dma(cst8[:, 1, :], self.w["gdn_a_log"][l].rearrange("d h -> (d h)").partition_broadcast(128), w=[bc8])
            P.actf(cst8[:, 1, :], cst8[:, 1, :], AF.Exp, [bc8], [bc8])
            P.ts("dve", cst8[:, 1, :], cst8[:, 1, :], -1.0, None, ALU.mult, None, [bc8], [bc8])
            P.tt("dve", gcol[:], self.gab[:, :, 0:8], cst8[:, 0:1, :].to_broadcast([128, NTL, 8]), ALU.add,
                 [self.bgab, bc8], [bg])
            P.actf(gcol[:], gcol[:], AF.Exp, [bg], [bg])
            P.actf(gcol[:], gcol[:], AF.Ln, [bg], [bg], scale=1.0, bias=1.0)
            P.tt("dve", gcol[:], gcol[:], cst8[:, 1:2, :].to_broadcast([128, NTL, 8]), ALU.mult, [bg, bc8], [bg])
            P.actf(beta[:], self.gab[:, :, 8:16], AF.Sigmoid, [self.bgab], [bbe])
            P.ts("dve", nbeta[:], beta[:], -1.0, None, ALU.mult, None, [bbe], [bnb])
            parts = self.cfg.get("gdn_parts", "pso")
            if "p" in parts:
                self.gdn_prep(l)
            if "s" in parts:
                self.gdn_scan(l, last, gcol, bg, beta, bbe, nbeta, bnb)
            if "o" in parts:
                self.gdn_out(l, last)

    def gdn_prep(self, l):
        P = self.P
        ident_b, bib = self.k["ident_b"]
        ones_b, bob = self.k["ones_b"]
        rperm, brp = self.k["rperm"]
        with P.scope():
            cwt, bcw = P.tile("gcw", [128, 5, 12], F32)
            with P.scope():
                self.lc_setup()
                for j in range(5):
                    self.load_cols(cwt[:, j, :], bcw, self.w["gdn_conv"][l][j].rearrange("(c p) -> c p", p=128), 12)
            dw, bdw = P.tile("gdw", [128, 12, 5, 128], BF16)
            for ch in range(12):
                for j in range(5):
                    P.ts("pool" if (ch + j) % 2 else "dve", dw[:, ch, j, :], ident_b[:], cwt[:, j, ch : ch + 1], None,
                         ALU.mult, None, [bib, bcw], [bdw])
            xbs = Rot(P, "gxb", 9, [128, 516], BF16)
            sxs = Rot(P, "gsx", 8, [128, 512], F32)
            sqs = Rot(P, "gsq", 8, [128, 512], BF16)
            rss = Rot(P, "grs", 8, [128, 512], F32)
            xns = Rot(P, "gxn", 16, [128, 512], BF16)
            t1s = Rot(P, "gt1", 8, [128, 512], F32)
            t2s = Rot(P, "gt2", 8, [128, 512], F32)
            cst = Rot(P, "gcs", 3, [128, 2, 512], F32)
            tks = Rot(P, "gtk", 3, [128, 4, 128], BF16)
            psC = psA = psB = FreeList(P, "gpC", 7, [128, 512], F32, psum=True)
            psT = Rot(P, "gpT", 1, [128, 4, 128], BF16, psum=True)

            def chunk(so, sl, s0, Wo, rope, cs_, bcs, ch):
                W = Wo + 4
                c0 = so + s0
                kind, h = ch // 4, ch % 4
                xb, bx = xbs.next()
                a0 = max(0, 2 - s0)
                a1 = min(W, sl - s0 + 2)
                P.dma(xb[:, a0:a1], self.pF[2 + ch, :, c0 - 2 + a0 : c0 - 2 + a1], w=[bx])
                if a0 > 0:
                    P.memset("pool", xb[:, 0:a0], 0.0, [bx])
                if a1 < W:
                    P.memset("pool", xb[:, a1:W], 0.0, [bx])
                pcI = psC.alloc()
                pc, bpc = pcI
                for j in range(5):
                    P.mm(pc[:, :Wo], dw[:, ch, j, :], xb[:, j : j + Wo], j == 0, j == 4, [bdw, bx], [bpc])
                yield
                xn, bxn = xns.next()
                if kind == 2:
                    P.actf(xn[:, :Wo], pc[:, :Wo], AF.Silu, [bpc], [bxn])
                    psC.release(pcI)
                else:
                    sx, bsx = sxs.next()
                    P.actf(sx[:, :Wo], pc[:, :Wo], AF.Silu, [bpc], [bsx])
                    psC.release(pcI)
                    sq, bsq = sqs.next()
                    P.actf(sq[:, :Wo], sx[:, :Wo], AF.Square, [bsx], [bsq])
                    psI = psA.alloc()
                    ps, bps = psI
                    P.mm(ps[:, :Wo], ones_b[:], sq[:, :Wo], True, True, [bob, bsq], [bps])
                    yield
                    rs, brs = rss.next()
                    P.actf(rs[:, :Wo], ps[:, :Wo], AF.Sqrt, [bps], [brs], scale=1.0, bias=EPS)
                    psA.release(psI)
                    P.recip(rs[:, :Wo], rs[:, :Wo], [brs], [brs])
                    qs = 128.0 ** -0.5 if kind == 0 else 1.0
                    if not rope:
                        P.stt("dve", xn[:, :Wo], sx[:, :Wo], qs, rs[:, :Wo], ALU.mult, ALU.mult, [bsx, brs], [bxn])
                    else:
                        xm, bxm = xns.next()
                        P.stt("dve", xm[:, :Wo], sx[:, :Wo], qs, rs[:, :Wo], ALU.mult, ALU.mult, [bsx, brs], [bxm])
                        prI = psB.alloc()
                        pr, bpr = prI
                        P.mm(pr[:, :Wo], rperm[:], xm[:, :Wo], True, True, [brp, bxm], [bpr])
                        yield
                        t1, bt1 = t1s.next()
                        t2, bt2 = t2s.next()
                        P.tt("pool", t1[:, :Wo], xm[:, :Wo], cs_[:, 0, :Wo], ALU.mult, [bxm, bcs], [bt1])
                        P.tt("dve", t2[:, :Wo], pr[:, :Wo], cs_[:, 1, :Wo], ALU.mult, [bpr, bcs], [bt2])
                        psB.release(prI)
                        P.tt("pool", xn[:, :Wo], t1[:, :Wo], t2[:, :Wo], ALU.add, [bt1, bt2], [bxn])
                yield
                if kind == 0:
                    P.dma(self.gQT[h, :, c0 : c0 + Wo], xn[:, :Wo], r=[bxn], q="pool")
                if kind == 1:
                    P.dma(self.gKT[h, :, c0 : c0 + Wo], xn[:, :Wo], r=[bxn], q="pool")
                if kind >= 1:
                    dst = self.gK if kind == 1 else self.gV
                    nb = Wo // 128
                    pt, bpt = psT.next()
                    for b in range(nb):
                        P.tr(pt[:, b, :], xn[:, b * 128 : (b + 1) * 128], ident_b[:], [bxn, bib], [bpt])
                    tk, btk = tks.next()
                    P.cp("act", tk[:, :nb, :], pt[:, :nb, :], [bpt], [btk])
                    P.dma(dst[c0 : c0 + Wo, h * 128 : (h + 1) * 128].rearrange("(b p) d -> p b d", p=128),
                          tk[:, :nb, :], r=[btk], q="act")

            tasks = []
            tabs = {}

            def mk(so, sl, s0, Wo, rope, ch):
                def run():
                    cs_ = bcs = None
                    if rope:
                        if s0 not in tabs:
                            cs_, bcs = cst.next()
                            P.dma(cs_[:, 0, :Wo], self.big["rope_cos"][:, s0 : s0 + Wo], w=[bcs])
                            P.dma(cs_[:, 1, :Wo], self.big["rope_sin"][:, s0 : s0 + Wo], w=[bcs])
                            tabs[s0] = (cs_, bcs)
                        cs_, bcs = tabs[s0]
                    yield from chunk(so, sl, s0, Wo, rope, cs_, bcs, ch)
                return run

            for (so, sl, rope) in ((0, CTX, False), (CTX, SEQ, True)):
                for s0 in range(0, sl, 512):
                    Wo = min(512, sl - s0)
                    for ch in range(12):
                        tasks.append(mk(so, sl, s0, Wo, rope, ch))
            rolling(tasks, 6)

    def gdn_scan(self, l, last, gcol, bg, beta, bbe, nbeta, bnb):
        P = self.P
        ident_b, bib = self.k["ident_b"]
        ident_f, bif = self.k["ident_f"]
        ones_f, bof = self.k["ones_f"]
        offd, bofd = self.k["offd"]
        NTL = NT // 128
        with P.scope():
            tri, btri = P.tile("gtri", [128, 2, 128], F32)
            negm, bnm = P.tile("gnegm", [128, 2, 128], F32)
            P.dma(tri[:], self.big["tri"].rearrange("d p c -> p d c"), w=[btri])
            P.dma(negm[:], self.big["negm"].rearrange("d p c -> p d c"), w=[bnm])
            R3 = lambda nm, dt, n=3: Rot(P, nm, n, [128, 4, 128], dt)
            NL = 3
            RD = []
            for d in range(NL):
                t = f"{d}"
                RD.append(dict(
                    pg=Rot(P, "gpg" + t, 2, [128, 4, 128], F32, psum=True),
                    q=R3("gq" + t, BF16, 1), k=R3("gk" + t, BF16, 1), km=R3("gkm" + t, BF16, 1), vm=R3("gvm" + t, BF16, 1),
                    rg=R3("grg" + t, F32, 1), dt=R3("gdt" + t, F32, 1), ea=R3("gea" + t, F32, 1), es=R3("ges" + t, F32, 1),
                    egr=R3("geg" + t, F32, 1),
                    m=R3("gm" + t, F32, 2), mt=R3("gmt" + t, F32, 2), pk=R3("gp" + t, F32, 2), tt=R3("gtt" + t, BF16, 1),
                    it=R3("git" + t, BF16, 3), wt=R3("gwt" + t, BF16, 3), qd=R3("gqd" + t, BF16, 3), kd=R3("gkd" + t, BF16, 3),
                    ke=R3("gke" + t, BF16, 1), ub=R3("gub" + t, F32, 3),
                    sm=Rot(P, "gsm" + t, 3, [128, 4, 4], F32),
                ))
            RS = [dict(vn=R3(f"gvn{d}", BF16, 1), ot=R3(f"got{d}", F32, 2)) for d in range(2)]
            pgs = Rot(P, "gpgs", 2, [128, 4, 128], F32, psum=True)
            S = [P.tile(f"gS{d}", [128, 4, 128], F32) for d in range(2)]
            Sb = [P.tile(f"gSb{d}", [128, 4, 128], BF16) for d in range(2)]
            for d in range(2):
                P.memset("pool", S[d][0][:], 0.0, [S[d][1]])
                P.memset("pool", Sb[d][0][:], 0.0, [Sb[d][1]])
            order = [list(range(NTL)), [1, 0] + list(range(NTL - 1, 1, -1))]
            evi = [0]

            def evac(out, in_, r, w):
                e = "act" if evi[0] % 2 == 0 else "dve"
                evi[0] += 1
                P.cp(e, out, in_, r, w)

            def pre(n, d, res, lane):
                R = RD[lane]
                pg = R["pg"]
                lastc = 127 if d == 0 else 0
                hs = slice(d * 4, d * 4 + 4)
                qt, bq = R["q"].next()
                kt, bk = R["k"].next()
                km, bkm = R["km"].next()
                vm, bvm = R["vm"].next()
                cs = slice(n * 128, (n + 1) * 128)
                P.dma(qt[:], self.gQT[:, :, cs].rearrange("h p t -> p h t"), w=[bq])
                P.dma(kt[:], self.gKT[:, :, cs].rearrange("h p t -> p h t"), w=[bk])
                P.dma(km[:], self.gK[cs, :].rearrange("p (h d) -> p h d", h=4), w=[bkm])
                P.dma(vm[:], self.gV[cs, :].rearrange("p (h d) -> p h d", h=4), w=[bvm])
                g4 = gcol[:, n, hs]
                rg, brg = R["rg"].next()
                P.tt("pool", rg[:], tri[:, d : d + 1, :].to_broadcast([128, 4, 128]), g4.unsqueeze(2).to_broadcast([128, 4, 128]),
                     ALU.mult, [btri, bg], [brg])
                gcr, bgcr = pg.next()
                P.mm(gcr[:].rearrange("p h c -> p (h c)"), ones_f[:], rg[:].rearrange("p h c -> p (h c)"), True, True,
                     [bof, brg], [bgcr])
                gcc, bgcc = pg.next()
                P.mm(gcc[:, 0, 0:4], tri[:, d, :], g4, True, True, [btri, bg], [bgcc])
                yield
                sm, bsm = R["sm"].next()
                gcs, egc, kdsc, glast = sm[:, 0, :], sm[:, 1, :], sm[:, 2, :], sm[:, 3, :]
                P.cp("dve", gcs, gcc[:, 0, 0:4], [bgcc], [bsm])
                P.actf(egc, gcs, AF.Exp, [bsm], [bsm])
                P.tt("dve", kdsc, gcr[:, :, lastc], gcs, ALU.subtract, [bgcr, bsm], [bsm])
                P.actf(kdsc, kdsc, AF.Exp, [bsm], [bsm])
                P.actf(glast, gcr[:, :, lastc], AF.Exp, [bgcr], [bsm])
                yield
                egr, begr = R["egr"].next()
                P.actf(egr[:], gcr[:], AF.Exp, [bgcr], [begr])
                dt_, bdt = R["dt"].next()
                P.tt("dve", dt_[:], gcr[:], gcs.unsqueeze(2).to_broadcast([128, 4, 128]), ALU.subtract, [bgcr, bsm], [bdt])
                P.tt("dve", dt_[:], dt_[:], negm[:, d : d + 1, :].to_broadcast([128, 4, 128]), ALU.min, [bdt, bnm], [bdt])
                kk, bkk = pg.next()
                for h in range(4):
                    P.mm(kk[:, h, :], kt[:, h, :], kt[:, h, :], True, True, [bk], [bkk])
                yield
                ea, bea = R["ea"].next()
                P.actf(ea[:], dt_[:], AF.Exp, [bdt], [bea])
                es, bes = R["es"].next()
                P.tt("pool", es[:], ea[:], offd[:].unsqueeze(1).to_broadcast([128, 4, 128]), ALU.mult, [bea, bofd], [bes])
                yield
                m, bm = R["m"].next()
                for h in range(4):
                    P.stt("dve", m[:, h, :], kk[:, h, :], nbeta[:, n, d * 4 + h : d * 4 + h + 1], es[:, h, :], ALU.mult, ALU.mult,
                          [bkk, bnb, bes], [bm])
                kq, bkq = pg.next()
                for h in range(4):
                    P.mm(kq[:, h, :], kt[:, h, :], qt[:, h, :], True, True, [bk, bq], [bkq])
                yield
                it, bit = R["it"].next()
                P.tt("dve", it[:], kq[:], ea[:], ALU.mult, [bkq, bea], [bit])
                ptr, bptr = pg.next()
                for h in range(4):
                    P.tr(ptr[:, h, :], m[:, h, :], ident_f[:], [bm, bif], [bptr])
                pk, bpk = R["pk"].next()
                P.tt("pool", pk[:], m[:], ident_f[:].unsqueeze(1).to_broadcast([128, 4, 128]), ALU.add, [bm, bif], [bpk])
                yield
                mt, bmt = R["mt"].next()
                P.cp("act", mt[:], ptr[:], [bptr], [bmt])
                yield
                for lev in range(6):
                    if lev < 5:
                        pm, bpm = pg.next()
                        for h in range(4):
                            P.mm(pm[:, h, :], mt[:, h, :], m[:, h, :], True, True, [bmt, bm], [bpm])
                        yield
                        m2, bm2 = R["m"].next()
                        evac(m2[:], pm[:], [bpm], [bm2])
                        yield
                        pmt, bpmt = pg.next()
                        for h in range(4):
                            P.tr(pmt[:, h, :], m2[:, h, :], ident_f[:], [bm2, bif], [bpmt])
                    else:
                        pmt, bpmt = pg.next()
                        for h in range(4):
                            P.mm(pmt[:, h, :], m[:, h, :], mt[:, h, :], True, True, [bmt, bm], [bpmt])
                    yield
                    mt2, bmt2 = R["mt"].next()
                    P.cp("act", mt2[:], pmt[:], [bpmt], [bmt2])
                    yield
                    pp, bpp = pg.next()
                    for h in range(4):
                        P.mm(pp[:, h, :], mt2[:, h, :], pk[:, h, :], True, True, [bmt2, bpk], [bpp])
                    yield
                    pk2, bpk2 = R["pk"].next()
                    P.tt("dve", pk2[:], pp[:], pk[:], ALU.add, [bpp, bpk], [bpk2])
                    pk, bpk = pk2, bpk2
                    mt, bmt = mt2, bmt2
                    if lev < 5:
                        m, bm = m2, bm2
                    yield
                tt_, btt = R["tt"].next()
                P.cp("act", tt_[:], pk[:], [bpk], [btt])
                ke, bke = R["ke"].next()
                P.tt("pool", ke[:], km[:], egc.unsqueeze(2).to_broadcast([128, 4, 128]), ALU.mult, [bkm, bsm], [bke])
                yield
                up, bup = pg.next()
                for h in range(4):
                    P.mm(up[:, h, :], tt_[:, h, :], vm[:, h, :], True, True, [btt, bvm], [bup])
                wp, bwp = pg.next()
                for h in range(4):
                    P.mm(wp[:, h, :], ke[:, h, :], tt_[:, h, :], True, True, [bke, btt], [bwp])
                qd, bqd = R["qd"].next()
                P.tt("pool", qd[:], qt[:], egr[:], ALU.mult, [bq, begr], [bqd])
                kd, bkd = R["kd"].next()
                P.tt("pool", kd[:], km[:], kdsc.unsqueeze(2).to_broadcast([128, 4, 128]), ALU.mult, [bkm, bsm], [bkd])
                yield
                ub, bub = R["ub"].next()
                P.tt("dve", ub[:], up[:], beta[:, n, hs].unsqueeze(2).to_broadcast([128, 4, 128]), ALU.mult, [bup, bbe], [bub])
                wt, bwt = R["wt"].next()
                P.cp("act", wt[:], wp[:], [bwp], [bwt])
                res.update(n=n, d=d, it=(it, bit), wt=(wt, bwt), qd=(qd, bqd), kd=(kd, bkd), ub=(ub, bub), glast=(glast, bsm))

            def step(r):
                n, d = r["n"], r["d"]
                R = RS[d]
                s_, bs = S[d]
                sb_, bsb = Sb[d]
                it, bit = r["it"]
                wt, bwt = r["wt"]
                qd, bqd = r["qd"]
                kd, bkd = r["kd"]
                ub, bub = r["ub"]
                glast, bgl = r["glast"]
                ws, bws = pgs.next()
                for h in range(4):
                    P.mm(ws[:, h, :], wt[:, h, :], sb_[:, h, :], True, True, [bwt, bsb], [bws])
                yield
                vn, bvn = R["vn"].next()
                for h in range(4):
                    P.stt("dve", vn[:, h, :], ws[:, h, :], nbeta[:, n, d * 4 + h : d * 4 + h + 1], ub[:, h, :], ALU.mult, ALU.add,
                          [bws, bnb, bub], [bvn])
                P.tt("pool", s_[:], s_[:], glast.unsqueeze(2).to_broadcast([128, 4, 128]), ALU.mult, [bs, bgl], [bs])
                yield
                dsp, bds = pgs.next()
                for h in range(4):
                    P.mm(dsp[:, h, :], kd[:, h, :], vn[:, h, :], True, True, [bkd, bvn], [bds])
                yield
                P.tt("dve", s_[:], s_[:], dsp[:], ALU.add, [bs, bds], [bs])
                if not (last and n < 2):
                    op_, bop = pgs.next()
                    for h in range(4):
                        P.mm(op_[:, h, :], qd[:, h, :], sb_[:, h, :], True, False, [bqd, bsb], [bop])
                        P.mm(op_[:, h, :], it[:, h, :], vn[:, h, :], False, True, [bit, bvn], [bop])
                    yield
                    ot, bot = R["ot"].next()
                    P.cp("act", ot[:], op_[:], [bop], [bot])
                    P.dma(self.gO[d][n * 128 : (n + 1) * 128, :].rearrange("p (h d) -> p h d", h=4), ot[:], r=[bot], q="act")
                P.cp("act", sb_[:], s_[:], [bs], [bsb])

            done = {}
            stepped = set()

            def pre_lane(r):
                for _ in range(r * 14):
                    yield
                for j in range(r, 2 * NTL, NL):
                    i, d = j // 2, j % 2
                    while j - 2 * NL >= 0 and (j - 2 * NL) not in stepped:
                        yield
                    res = {}
                    yield from pre(order[d][i], d, res, r)
                    done[j] = res

            def step_lane(d):
                for i in range(NTL):
                    j = 2 * i + d
                    while j not in done:
                        yield
                    yield from step(done[j])
                    stepped.add(j)

            interleave([pre_lane(r) for r in range(NL)] + [step_lane(0), step_lane(1)])

    def gdn_out(self, l, last):
        P = self.P
        ident_b, bib = self.k["ident_b"]
        with P.scope():
            nw, bnw = P.tile("gnw", [128, 128], F32)
            P.dma(nw[:], self.w["gdn_norm"][l].partition_broadcast(128), w=[bnw])
            ofs = Rot(P, "gof", 5, [128, 4, 128], F32)
            obs = Rot(P, "gob", 5, [128, 4, 128], F32)
            zs = Rot(P, "gz", 5, [128, 4, 128], BF16)
            szs = Rot(P, "gsz", 5, [128, 4, 128], F32)
            sq2 = Rot(P, "gsq2", 5, [128, 4, 128], F32)
            sss = Rot(P, "gss", 5, [128, 4], F32)
            ybs = Rot(P, "gyb", 5, [128, 4, 128], BF16)
            yts = Rot(P, "gyt", 5, [128, 4, 128], BF16)
            psT = Rot(P, "gpT2", 4, [128, 4, 128], BF16, psum=True)
            def otile(n):
                rows = slice(n * 128, (n + 1) * 128)
                of, bof_ = ofs.next()
                ob, bob_ = obs.next()
                z, bz = zs.next()
                P.dma(of[:], self.gO[0][rows, :].rearrange("p (h d) -> p h d", h=4), w=[bof_])
                P.dma(ob[:], self.gO[1][rows, :].rearrange("p (h d) -> p h d", h=4), w=[bob_])
                P.dma(z[:], self.pT[rows, 0:512].rearrange("p (h d) -> p h d", h=4), w=[bz])
                P.tt("pool", of[:], of[:], ob[:], ALU.add, [bof_, bob_], [bof_])
                sq, bsq = sq2.next()
                P.tt("pool", sq[:], of[:], of[:], ALU.mult, [bof_], [bsq])
                ss, bss = sss.next()
                P._add("dve", lambda e, ss=ss, sq=sq: e.reduce_sum(out=ss[:], in_=sq[:], axis=AX.X), [bsq], [bss])
                yield
                P.actf(ss[:], ss[:], AF.Sqrt, [bss], [bss], scale=1.0 / 128.0, bias=EPS)
                P.recip(ss[:], ss[:], [bss], [bss])
                P.tt("dve", of[:], of[:], ss[:].unsqueeze(2).to_broadcast([128, 4, 128]), ALU.mult, [bof_, bss], [bof_])
                P.tt("pool", of[:], of[:], nw[:].unsqueeze(1).to_broadcast([128, 4, 128]), ALU.mult, [bof_, bnw], [bof_])
                yield
                sz, bsz = szs.next()
                P.actf(sz[:], z[:], AF.Silu, [bz], [bsz])
                yb, byb = ybs.next()
                P.tt("dve", yb[:], of[:], sz[:], ALU.mult, [bof_, bsz], [byb])
                pt, bpt = psT.next()
                for h in range(4):
                    P.tr(pt[:, h, :], yb[:, h, :], ident_b[:], [byb, bib], [bpt])
                yield
                yt, byt = yts.next()
                P.cp("act", yt[:], pt[:], [bpt], [byt])
                P.dma(fm(self.yT)[:, 2:6, n * 128 : (n + 1) * 128], yt[:], r=[byt], q="act")

            tl = list(range(2 if last else 0, NT // 128))
            for i in range(0, len(tl), 4):
                interleave([otile(n) for n in tl[i : i + 4]])


_CACHE = {}


def kernel(**inputs):
    cfg = {}
    if "full" not in _CACHE:
        _CACHE["full"] = Kern(cfg).build()
    nc = _CACHE["full"]
    consts = host_consts()
    rpb = np.asarray(inputs["na_rpb"], dtype=np.float32)
    nab = np.stack([na_bias_sets(rpb[l]) for l in range(DEPTH)]).reshape(DEPTH, 5, 128, 4 * 5 * 128)
    in_maps = []
    for b in range(8):
        m = {k: np.ascontiguousarray(inputs[k], dtype=np.float32) for k in W_SPECS}
        m["x"] = np.ascontiguousarray(inputs["x"][b], dtype=np.float32)
        m["ctx"] = np.ascontiguousarray(inputs["ctx"][b], dtype=np.float32)
        m["c"] = np.ascontiguousarray(inputs["c"][b], dtype=np.float32)
        m["c_ctx"] = np.ascontiguousarray(inputs["c_ctx"], dtype=np.float32)
        m.update(consts)
        m["na_bias"] = nab
        in_maps.append(m)

    res = run_bass_kernel_spmd(nc, in_maps, core_ids=list(range(8)))
    return np.stack([np.asarray(r["out"], dtype=np.float32) for r in res.results], axis=0)
```

```python
import math
from contextlib import ExitStack, contextmanager

import ml_dtypes
import numpy as np

import concourse.bass as bass
import concourse.mybir as mybir
from concourse.bass_utils import run_bass_kernel_spmd

F32 = mybir.dt.float32
BF16 = mybir.dt.bfloat16
ALU = mybir.AluOpType
AF = mybir.ActivationFunctionType
AX = mybir.AxisListType

D = 1024
SEQ = 4096
CTX = 256
NT = CTX + SEQ
DEPTH = 4
NIN = 3088
DFF = 2816
EPS = 1e-6
GRID = 64
NEG = -30000.0

NDMA_SEM = 8


class Buf:
    __slots__ = ("name", "w", "r", "excl")

    def __init__(self, name, excl=False):
        self.name = name
        self.w = None
        self.r = []
        self.excl = excl


class Op:
    __slots__ = ("eng", "fn", "deps", "sig", "semval", "dma", "dsem", "dval", "dprev")

    def __init__(self, eng, fn, dma):
        self.eng = eng
        self.fn = fn
        self.deps = []
        self.sig = False
        self.semval = 0
        self.dma = dma
        self.dsem = None
        self.dval = 0
        self.dprev = None


class Prog:
    ENGS = ("pe", "act", "dve", "pool", "sp")

    def __init__(self, nc):
        self.nc = nc
        self.ops = {e: [] for e in self.ENGS}
        self.gstack = ExitStack()
        self.stack = self.gstack
        self.ndma = {e: 0 for e in self.ENGS}
        self.dma_last = {}
        self.nbuf = 0
        self.out_dmas = []
        self.bar = {e: [] for e in self.ENGS}
        self.nname = 0

    def sb(self, name, shape, dt):
        self.nname += 1
        return self.stack.enter_context(self.nc.sbuf_tensor(f"{name}_{self.nname}", list(shape), dt))

    def ps(self, name, shape, dt=F32):
        self.nname += 1
        t = self.stack.enter_context(self.nc.psum_tensor(f"{name}_{self.nname}", list(shape), dt))
        return t

    def buf(self, name=None):
        self.nbuf += 1
        return Buf(name or f"b{self.nbuf}")

    def tile(self, name, shape, dt):
        return self.sb(name, shape, dt), self.buf(name)

    def ptile(self, name, shape, dt=F32):
        b = self.buf(name)
        b.excl = True
        return self.ps(name, shape, dt), b

    @contextmanager
    def scope(self):
        old = self.stack
        st = ExitStack()
        self.stack = st
        try:
            yield
        finally:
            self.barrier()
            st.close()
            self.stack = old

    def barrier(self):
        deps = []
        for e in self.ENGS:
            for op in reversed(self.ops[e]):
                if not op.dma:
                    deps.append(op)
                    break
        deps.extend(self.dma_last.values())
        for e in self.ENGS:
            self.bar[e] = list(deps)

    def _add(self, eng, fn, reads, writes, dma=False):
        op = Op(eng, fn, dma)
        deps = []
        ex = [b for b in reads if b.excl]
        if ex:
            reads = [b for b in reads if not b.excl]
            writes = list(writes) + ex
        for b in reads:
            if b.w is not None:
                deps.append((b.w, True))
        for b in writes:
            if b.w is not None:
                deps.append((b.w, b.excl))
            for x in b.r:
                deps.append((x, False))
        if self.bar[eng]:
            for x in self.bar[eng]:
                deps.append((x, True))
            self.bar[eng] = []
        seen = {}
        for d, raw in deps:
            seen[id(d)] = (d, seen.get(id(d), (d, False))[1] or raw)
        for d, raw in seen.values():
            if d is op:
                continue
            if (not d.dma) and d.eng == eng:
                if eng == "pe" or not raw:
                    continue
            op.deps.append(d)
            d.sig = True
        for b in reads:
            b.r.append(op)
        for b in writes:
            b.w = op
            b.r = []
        if dma:
            i = self.ndma[eng]
            self.ndma[eng] += 1
            slot = i % NDMA_SEM
            op.dsem = (eng, slot)
            op.dval = 16 * (i // NDMA_SEM + 1)
            op.dprev = self.dma_last.get((eng, slot))
            self.dma_last[(eng, slot)] = op
        self.ops[eng].append(op)
        return op

    def dma(self, out, in_, r=(), w=(), q="sp", is_out=False, **kw):
        op = self._add(q, lambda e: e.dma_start(out=out, in_=in_, **kw), r, w, dma=True)
        if is_out:
            self.out_dmas.append(op)
        return op

    def mm(self, out, lhsT, rhs, start, stop, r, w):
        return self._add("pe", lambda e: e.matmul(out, lhsT=lhsT, rhs=rhs, start=start, stop=stop), r, w)

    def tr(self, out, in_, ident, r, w):
        return self._add("pe", lambda e: e.transpose(out, in_, ident), r, w)

    def actf(self, out, in_, func, r, w, scale=1.0, bias=0.0):
        return self._add("act", lambda e: e.activation(out=out, in_=in_, func=func, bias=bias, scale=scale), r, w)

    def cp(self, eng, out, in_, r, w):
        if eng == "act":
            return self._add("act", lambda e: e.copy(out=out, in_=in_), r, w)
        return self._add(eng, lambda e: e.tensor_copy(out=out, in_=in_), r, w)

    def tt(self, eng, out, in0, in1, op, r, w):
        return self._add(eng, lambda e: e.tensor_tensor(out=out, in0=in0, in1=in1, op=op), r, w)

    def ts(self, eng, out, in0, s1, s2, op0, op1, r, w):
        if s2 is None:
            return self._add(eng, lambda e: e.tensor_scalar(out=out, in0=in0, scalar1=s1, scalar2=None, op0=op0), r, w)
        return self._add(eng, lambda e: e.tensor_scalar(out=out, in0=in0, scalar1=s1, scalar2=s2, op0=op0, op1=op1), r, w)

    def stt(self, eng, out, in0, scalar, in1, op0, op1, r, w):
        return self._add(
            eng, lambda e: e.scalar_tensor_tensor(out=out, in0=in0, scalar=scalar, in1=in1, op0=op0, op1=op1), r, w
        )

    def recip(self, out, in_, r, w):
        return self._add("dve", lambda e: e.reciprocal(out=out, in_=in_), r, w)

    def memset(self, eng, ap, val, w):
        return self._add(eng, lambda e: e.memset(ap, val), (), w)

    def emit(self):
        nc = self.nc
        st = self.gstack
        sems = {e: st.enter_context(nc.semaphore(f"s_{e}")) for e in self.ENGS}
        dsems = {}
        for e in self.ENGS:
            for k in range(min(NDMA_SEM, self.ndma[e])):
                dsems[(e, k)] = st.enter_context(nc.semaphore(f"d_{e}{k}"))
        for e in self.ENGS:
            c = 0
            for op in self.ops[e]:
                if op.dma:
                    continue
                if op.sig:
                    c += 1
                    op.semval = c
        out_dmas = self.out_dmas
        engmap = {"pe": "tensor", "act": "scalar", "dve": "vector", "pool": "gpsimd", "sp": "sync"}
        stats = {}

        def run(ename, eng):
            known = {}
            nw = 0

            def wait(key, sem, val):
                nonlocal nw
                if known.get(key, 0) >= val:
                    return
                known[key] = val
                eng.wait_ge(sem, val)
                nw += 1

            for op in self.ops[ename]:
                for d in op.deps:
                    if d.dma:
                        wait(d.dsem, dsems[d.dsem], d.dval)
                    else:
                        wait(d.eng, sems[d.eng], d.semval)
                if op.dma and op.dprev is not None:
                    wait(op.dsem, dsems[op.dsem], op.dprev.dval)
                ins = op.fn(eng)
                if op.dma:
                    ins.then_inc(dsems[op.dsem], 16)
                elif op.sig:
                    ins.then_inc(sems[ename], 1)
            if ename == "sp":
                for d in out_dmas:
                    wait(d.dsem, dsems[d.dsem], d.dval)
            stats[ename] = (len(self.ops[ename]), nw)

        with nc.Block() as block:
            for ename in self.ENGS:
                if not self.ops[ename] and ename != "sp":
                    continue
                getattr(block, engmap[ename])(lambda eng, ename=ename: run(ename, eng))
        self.stats = stats
        st.close()
        return nc


def interleave(gens):
    gens = list(gens)
    while gens:
        for g in list(gens):
            try:
                next(g)
            except StopIteration:
                gens.remove(g)


class FreeList:
    def __init__(self, P, name, n, shape, dt, psum=False):
        self.free = []
        for i in range(n):
            t = P.ps(f"{name}{i}", shape, dt) if psum else P.sb(f"{name}{i}", shape, dt)
            b = P.buf(f"{name}{i}")
            b.excl = psum
            self.free.append((t, b))

    def alloc(self):
        assert self.free, "FreeList exhausted"
        return self.free.pop(0)

    def release(self, item):
        self.free.append(item)


def rolling(tasks, width, stagger=2):
    tasks = list(tasks)
    nxt = [0]

    def lane(r):
        for _ in range(r * stagger):
            yield
        while nxt[0] < len(tasks):
            t = tasks[nxt[0]]
            nxt[0] += 1
            yield from t()

    interleave([lane(r) for r in range(width)])


class Rot:
    def __init__(self, P, name, n, shape, dt, psum=False):
        self.items = []
        for i in range(n):
            t = P.ps(f"{name}{i}", shape, dt) if psum else P.sb(f"{name}{i}", shape, dt)
            b = P.buf(f"{name}{i}")
            b.excl = psum
            self.items.append((t, b))
        self.i = 0

    def next(self):
        it = self.items[self.i % len(self.items)]
        self.i += 1
        return it


def host_consts():
    c = {}
    bf = ml_dtypes.bfloat16
    eye = np.eye(128, dtype=np.float32)
    c["ident_f"] = eye
    c["ident_b"] = eye.astype(bf)
    c["ones_b"] = np.ones((128, 128), bf)
    c["ones_f"] = np.ones((128, 128), np.float32)
    bd = np.zeros((128, 128), np.float32)
    bd[:64, :64] = 1
    bd[64:, 64:] = 1
    c["bd_b"] = bd.astype(bf)
    lo = np.zeros((128, 128), np.float32)
    lo[:, :64] = 1
    c["ones_lo"] = lo.astype(bf)
    c["ones_hi"] = (1 - lo).astype(bf)
    ii = np.arange(128)
    tri_f = (ii[:, None] <= ii[None, :]).astype(np.float32)
    c["tri"] = np.stack([tri_f, tri_f.T.copy()])
    c["negm"] = np.stack([np.where(ii[None, :] >= ii[:, None], 0.0, NEG), np.where(ii[None, :] <= ii[:, None], 0.0, NEG)]).astype(np.float32)
    c["offd"] = (1.0 - eye).astype(np.float32)
    t = np.arange(SEQ)
    freqs = (np.float32(10000.0) ** (-(np.arange(32, dtype=np.float32)) / np.float32(32))).astype(np.float32)
    cos_t = np.zeros((128, SEQ), np.float32)
    sin_t = np.zeros((128, SEQ), np.float32)
    rp = np.zeros((128, 128), np.float32)
    for i in range(128):
        pos = (t // GRID if i < 64 else t % GRID).astype(np.float32)
        ang = (pos * freqs[i % 32]).astype(np.float32)
        first = (i % 64) < 32
        cos_t[i] = np.cos(ang)
        sin_t[i] = (-1.0 if first else 1.0) * np.sin(ang)
        rp[i + 32 if first else i - 32, i] = 1.0
    c["rope_cos"] = cos_t
    c["rope_sin"] = sin_t
    c["rperm"] = rp.astype(bf)
    for name, Ls in (("invc_lat", SEQ), ("invc_ctx", CTX)):
        t = np.arange(Ls)
        tab = np.zeros((2, 128, Ls), np.float32)
        for g, win in enumerate((2, 4, 8, 16)):
            lo_ = np.clip(t - win // 2, 0, Ls)
            hi_ = np.clip(t + win - win // 2, 0, Ls)
            tab[g // 2, 64 * (g % 2) : 64 * (g % 2) + 64, :] = (1.0 / (hi_ - lo_).astype(np.float32))[None, :]
        c[name] = tab
    return c


NA_CLASSES = {0: 1, 1: 2, 30: 3, 31: 4}


def na_chunks(R):
    rows = set()
    for r in (2 * R, 2 * R + 1):
        r0 = min(max(r - 4, 0), GRID - 8)
        rows.update(range(r0, r0 + 8))
    return sorted({kr // 2 for kr in rows})


def na_bias_sets(rpb):
    H = rpb.shape[0]
    out = np.full((5, 128, H, 5, 128), NEG, np.float32)
    kc = np.arange(64)[:, None]
    qc = np.arange(64)[None, :]
    c0 = np.clip(qc - 8, 0, GRID - 16)
    cvis = (kc >= c0) & (kc < c0 + 16)
    dc = np.clip(kc - qc + 15, 0, 30)
    for cls, R in ((0, 10), (1, 0), (2, 1), (3, 30), (4, 31)):
        for slot, m in enumerate(na_chunks(R)):
            for kj in range(2):
                kr = 2 * m + kj
                for qi in range(2):
                    r = 2 * R + qi
                    r0 = min(max(r - 4, 0), GRID - 8)
                    if not (r0 <= kr < r0 + 8):
                        continue
                    dr = kr - r + 7
                    blk = rpb[:, dr][:, dc]
                    blk = np.where(cvis[None], blk, np.float32(NEG))
                    out[cls, 64 * kj : 64 * kj + 64, :, slot, 64 * qi : 64 * qi + 64] = blk.transpose(1, 0, 2)
    return out


CONST_SPECS = {
    "ident_f": ([128, 128], F32),
    "ident_b": ([128, 128], BF16),
    "ones_b": ([128, 128], BF16),
    "ones_f": ([128, 128], F32),
    "bd_b": ([128, 128], BF16),
    "ones_lo": ([128, 128], BF16),
    "ones_hi": ([128, 128], BF16),
    "offd": ([128, 128], F32),
    "rperm": ([128, 128], BF16),
}
BIG_CONSTS = {
    "invc_lat": ([2, 128, SEQ], F32),
    "invc_ctx": ([2, 128, CTX], F32),
    "na_bias": ([DEPTH, 5, 128, 4 * 5 * 128], F32),
    "tri": ([2, 128, 128], F32),
    "negm": ([2, 128, 128], F32),
    "rope_cos": ([128, SEQ], F32),
    "rope_sin": ([128, SEQ], F32),
}

W_SPECS = {
    "w_ada": [DEPTH, D, 6 * D],
    "b_ada": [DEPTH, 6 * D],
    "norm_mix": [DEPTH, D],
    "w_in": [DEPTH, D, NIN],
    "pool_w": [DEPTH, 4, 64, 64],
    "pool_scale": [DEPTH, 256],
    "gdn_conv": [DEPTH, 5, 1536],
    "gdn_a_log": [DEPTH, 2, 4],
    "gdn_dt_bias": [DEPTH, 2, 4],
    "gdn_norm": [DEPTH, 128],
    "na_q_norm": [DEPTH, 64],
    "na_k_norm": [DEPTH, 64],
    "na_rpb": [DEPTH, 4, 15, 31],
    "w_out": [DEPTH, D, D],
    "norm_ffn": [DEPTH, D],
    "w_up": [DEPTH, D, 2 * DFF],
    "ffn_conv": [DEPTH, 3, 2 * DFF],
    "w_down": [DEPTH, DFF, D],
}

F_COLS = [0, 128] + [256 + 128 * i for i in range(12)] + [2320 + 128 * i for i in range(4)]
NF = len(F_COLS)


def fm(ap3):
    return ap3.rearrange("k p t -> p k t")


class Kern:
    def __init__(self, cfg):
        self.cfg = cfg
        nc = bass.Bass("TRN2", target_bir_lowering=False)
        self.nc = nc
        self.P = Prog(nc)
        ext = cfg.get("ext", {})

        def dram(name, shape, dt, kind="Internal"):
            kind = ext.get(name, kind)
            return nc.dram_tensor(name, list(shape), dt, kind=kind).ap()

        self.x = dram("x", [SEQ, D], F32, "ExternalInput")
        self.ctx = dram("ctx", [CTX, D], F32, "ExternalInput")
        self.c = dram("c", [D], F32, "ExternalInput")
        self.c_ctx = dram("c_ctx", [D], F32, "ExternalInput")
        self.w = {k: dram(k, s, F32, "ExternalInput") for k, s in W_SPECS.items()}
        self.cst = {k: dram(k, s, dt, "ExternalInput") for k, (s, dt) in CONST_SPECS.items()}
        self.big = {k: dram(k, s, dt, "ExternalInput") for k, (s, dt) in BIG_CONSTS.items()}
        self.out = dram("out", [SEQ, D], F32, "ExternalOutput")
        self.resT = dram("resT", [8, 128, NT], F32)
        self.pF = dram("pF", [NF, 128, NT], BF16)
        self.pT = dram("pT", [NT, 768], BF16)
        self.yT = dram("yT", [8, 128, NT], BF16)
        self.hfT = dram("hfT", [8, 128, NT], BF16)
        self.gQT = dram("gQT", [4, 128, NT], BF16)
        self.gKT = dram("gKT", [4, 128, NT], BF16)
        self.gK = dram("gK", [NT, 512], BF16)
        self.gV = dram("gV", [NT, 512], BF16)
        self.gO = [dram("gOf", [NT, 512], F32), dram("gOb", [NT, 512], F32)]

    def load_consts(self):
        P = self.P
        self.k = {}
        for name, (shape, dt) in CONST_SPECS.items():
            t, b = P.tile("c_" + name, shape, dt)
            P.dma(t[:], self.cst[name][:, :], w=[b])
            self.k[name] = (t, b)
        self.mods = []
        for par in range(2):
            self.mods.append(dict(modv=P.tile(f"modv{par}", [128, 48, 2], F32), A1=P.tile(f"A1{par}", [128, 8, 2], F32),
                                  A2=P.tile(f"A2{par}", [128, 8, 2], F32)))
        self.set_layer(0)
        self.scT, self.bsc = P.tile("scT", [128, 8, 2], F32)
        with P.scope():
            self.lc_setup()
            self.load_cols(self.scT[:, :, 0], self.bsc, self.c.rearrange("(k p) -> k p", p=128), 8)
            self.load_cols(self.scT[:, :, 1], self.bsc, self.c_ctx.rearrange("(k p) -> k p", p=128), 8)
        P.actf(self.scT[:], self.scT[:], AF.Silu, [self.bsc], [self.bsc])

    def set_layer(self, l):
        m = self.mods[l % 2]
        self.modv, self.bmod = m["modv"]
        self.A1, self.bA1 = m["A1"]
        self.A2, self.bA2 = m["A2"]

    def lc_setup(self):
        self.lc_in = Rot(self.P, "lcin", 2, [128, 128], F32)
        self.lc_ps = self.P.ptile("lcps", [128, 128])

    def load_cols(self, dst, bdst, src2d, n):
        P = self.P
        ident, bi = self.k["ident_f"]
        ti, bti = self.lc_in.next()
        P.dma(ti[:n, :], src2d, w=[bti])
        pt, bp = self.lc_ps
        P.tr(pt[:, :n], ti[:n, :], ident[:n, :n], [bti, bi], [bp])
        P.cp("dve", dst, pt[:, :n], [bp], [bdst])

    def load_w_bf16(self, dst, src, K, N, stg, CH, order="col"):
        P = self.P
        bufs = {}
        nch = (N + CH - 1) // CH
        idx = [(k, c) for c in range(nch) for k in range(K)] if order == "col" else [(k, c) for k in range(K) for c in range(nch)]
        engs = ("pool", "act", "dve")
        for i, (k, c) in enumerate(idx):
            c0, c1 = c * CH, min(N, (c + 1) * CH)
            st, bs = stg.next()
            b = P.buf(f"w{k}_{c}")
            bufs[(k, c)] = b
            P.dma(st[:, : c1 - c0], src[k * 128 : (k + 1) * 128, c0:c1], w=[bs])
            P.cp(engs[i % 3], dst[:, k, c0:c1], st[:, : c1 - c0], [bs], [b])
        return bufs

    def phase_init(self):
        P = self.P
        ident, bi = self.k["ident_f"]
        with P.scope():
            xin = Rot(P, "xin", 2, [128, D], F32)
            pst = Rot(P, "pst", 2, [128, 8, 128], F32, psum=True)
            xo = Rot(P, "xo", 2, [128, 8, 128], F32)
            for s in range(NT // 128):
                t0 = s * 128
                src = self.ctx[t0 : t0 + 128, :] if t0 < CTX else self.x[t0 - CTX : t0 - CTX + 128, :]
                xt, bx = xin.next()
                P.dma(xt[:], src, w=[bx])
                pt, bp = pst.next()
                for k in range(8):
                    P.tr(pt[:, k, :], xt[:, k * 128 : (k + 1) * 128], ident[:], [bx, bi], [bp])
                ot, bo = xo.next()
                P.cp("act" if s % 2 else "dve", ot[:], pt[:], [bp], [bo])
                P.dma(fm(self.resT)[:, :, t0 : t0 + 128], ot[:], r=[bo], q="pool")

    def phase_ada(self, l):
        with self.P.scope():
            for _ in self.ada_gen(l):
                pass

    def ada_gen(self, l):
        P = self.P
        w_ada = self.w["w_ada"][l]
        m = self.mods[l % 2]
        modv, bmod = m["modv"]
        self.lc_setup()
        stg = Rot(P, "adast", 2, [128, 8, 1024], F32)
        pm, bpm = P.ptile("pmod", [128, 48, 2])
        bt, bb = P.tile("bada", [128, 48], F32)
        nt_, bn = P.tile("nrm", [128, 8, 2], F32)
        self.load_cols(bt[:], bb, self.w["b_ada"][l].rearrange("(j p) -> j p", p=128), 48)
        self.load_cols(nt_[:, :, 0], bn, self.w["norm_mix"][l].rearrange("(k p) -> k p", p=128), 8)
        self.load_cols(nt_[:, :, 1], bn, self.w["norm_ffn"][l].rearrange("(k p) -> k p", p=128), 8)
        for g in range(6):
            st, bs = stg.next()
            P.dma(st[:], w_ada[:, g * 1024 : (g + 1) * 1024].rearrange("(k p) n -> p k n", p=128), w=[bs])
            for n in range(8):
                for k in range(8):
                    P.mm(pm[:, g * 8 + n, :], st[:, k, n * 128 : (n + 1) * 128], self.scT[:, k, :],
                         k == 0, k == 7, [bs, self.bsc], [bpm])
                if n % 4 == 3:
                    yield
        P.tt("dve", modv[:], pm[:], bt[:].unsqueeze(2).to_broadcast([128, 48, 2]), ALU.add, [bpm, bb], [bmod])
        for (key, sidx, nidx) in (("A1", 1, 0), ("A2", 4, 1)):
            A, bA = m[key]
            P.ts("dve", A[:], modv[:, sidx * 8 : sidx * 8 + 8, :], 1.0, 32.0, ALU.add, ALU.mult, [bmod], [bA])
            P.tt("dve", A[:], A[:], nt_[:, :, nidx : nidx + 1].to_broadcast([128, 8, 2]), ALU.mult, [bA, bn], [bA])

    def mod(self, j, k, col):
        return self.modv[:, j * 8 + k, col : col + 1]

    def mk_norm_tiles(self):
        P = self.P
        nt = {}
        nt["sq"] = P.tile("sq", [128, 8, 512], BF16)
        nt["ss"] = P.ptile("ss", [128, 512])
        nt["rs"] = P.tile("rs", [128, 512], F32)
        nt["t"] = P.tile("nt", [128, 8, 512], F32)
        return nt

    def norm_mod(self, nt, xt, bx, W, A, bA, shj, col, hT, bh):
        P = self.P
        ones, bo = self.k["ones_b"]
        sq, bsq = nt["sq"]
        ss, bss = nt["ss"]
        rs, brs = nt["rs"]
        t, bt = nt["t"]
        P.actf(sq[:, :, :W], xt[:, :, :W], AF.Square, [bx], [bsq])
        for k in range(8):
            P.mm(ss[:, :W], ones[:], sq[:, k, :W], k == 0, k == 7, [bo, bsq], [bss])
        P.actf(rs[:, :W], ss[:, :W], AF.Sqrt, [bss], [brs], scale=1.0, bias=1024.0 * EPS)
        P.recip(rs[:, :W], rs[:, :W], [brs], [brs])
        P.tt("dve", t[:, :, :W], xt[:, :, :W], rs[:, :W].unsqueeze(1).to_broadcast([128, 8, W]), ALU.mult,
             [bx, brs], [bt])
        for k in range(8):
            P.actf(hT[:, k, :W], t[:, k, :W], AF.Identity, [bt, bA, self.bmod], [bh],
                   scale=A[:, k, col : col + 1], bias=self.mod(shj, k, col))

    def tiles_A(self):
        tl = [(0, CTX, 1)]
        for j in range(SEQ // 512):
            tl.append((CTX + 512 * j, 512, 0))
        return tl

    def phase_A(self, l):
        P = self.P
        with P.scope():
            Win, _ = P.tile("Win", [128, 8, NIN], BF16)
            stg = Rot(P, "wst", 3, [128, 772], F32)
            nt = self.mk_norm_tiles()
            xin = Rot(P, "xa", 2, [128, 8, 512], F32)
            hts = Rot(P, "hT", 2, [128, 8, 512], BF16)
            psF = Rot(P, "psF", 3, [128, 512], F32, psum=True)
            psT1 = Rot(P, "psT1", 1, [128, 512], F32, psum=True)
            psT2 = Rot(P, "psT2", 1, [128, 272], F32, psum=True)
            ob = Rot(P, "obF", 2, [128, 6, 512], BF16)
            tb = Rot(P, "obT", 2, [128, 768], BF16)
            ei = 0
            wb = None

            def bW(c_lo, c_hi):
                return [wb[(k, c)] for k in range(8) for c in range(c_lo // 772, (c_hi - 1) // 772 + 1)]

            for (c0, W, col) in self.tiles_A():
                xt, bx = xin.next()
                P.dma(xt[:, :, :W], fm(self.resT)[:, :, c0 : c0 + W], w=[bx])
                if wb is None:
                    wb = self.load_w_bf16(Win, self.w["w_in"][l], 8, NIN, stg, 772)
                hT, bh = hts.next()
                self.norm_mod(nt, xt, bx, W, self.A1, self.bA1, 0, col, hT, bh)
                for g in range(3):
                    o, bo = ob.next()
                    for j in range(6):
                        ch = g * 6 + j
                        ps, bp = psF.next()
                        for k in range(8):
                            P.mm(ps[:, :W], Win[:, k, F_COLS[ch] : F_COLS[ch] + 128], hT[:, k, :W],
                                 k == 0, k == 7, [wb[(k, F_COLS[ch] // 772)], wb[(k, (F_COLS[ch] + 127) // 772)], bh], [bp])
                        P.cp("act" if ei % 2 else "dve", o[:, j, :W], ps[:, :W], [bp], [bo])
                        ei += 1
                    P.dma(fm(self.pF)[:, g * 6 : g * 6 + 6, c0 : c0 + W], o[:, :, :W], r=[bo], q="pool")
                for sbk in range(W // 128):
                    t0 = c0 + sbk * 128
                    p1, b1 = psT1.next()
                    p2, b2 = psT2.next()
                    for k in range(8):
                        P.mm(p1[:], hT[:, k, sbk * 128 : (sbk + 1) * 128], Win[:, k, 1792:2304],
                             k == 0, k == 7, [wb[(k, 2)], bh], [b1])
                    for k in range(8):
                        P.mm(p2[:, 0:256], hT[:, k, sbk * 128 : (sbk + 1) * 128], Win[:, k, 2832:3088],
                             k == 0, k == 7, [wb[(k, 3)], bh], [b2])
                    for k in range(8):
                        P.mm(p2[:, 256:272], hT[:, k, sbk * 128 : (sbk + 1) * 128], Win[:, k, 2304:2320],
                             k == 0, k == 7, [wb[(k, 2)], wb[(k, 3)], bh], [b2])
                    o, bo = tb.next()
                    P.cp("act", o[:, 0:512], p1[:], [b1], [bo])
                    P.cp("dve", o[:, 512:768], p2[:, 0:256], [b2], [bo])
                    P.cp("dve", self.gab[:, t0 // 128, :], p2[:, 256:272], [b2], [self.bgab])
                    P.dma(self.pT[t0 : t0 + 128, :], o[:], r=[bo], q="pool")

    def phase_C1(self, l, last):
        P = self.P
        with P.scope():
            Wo, _ = P.tile("Wout", [128, 8, D], BF16)
            stg = Rot(P, "wst", 3, [128, 512], F32)
            wb = None
            nt = self.mk_norm_tiles()
            xin = Rot(P, "xc", 2, [128, 8, 512], F32)
            yin = Rot(P, "yc", 2, [128, 8, 512], BF16)
            xms = Rot(P, "xm", 2, [128, 8, 512], F32)
            hts = Rot(P, "hfo", 2, [128, 8, 512], BF16)
            pso = Rot(P, "pso", 3, [128, 512], F32, psum=True)
            for (c0, W, col) in self.tiles_A():
                if last and col == 1:
                    continue
                xt, bx = xin.next()
                yt, by = yin.next()
                P.dma(xt[:, :, :W], fm(self.resT)[:, :, c0 : c0 + W], w=[bx])
                P.dma(yt[:, :, :W], fm(self.yT)[:, :, c0 : c0 + W], w=[by])
                if wb is None:
                    wb = self.load_w_bf16(Wo, self.w["w_out"][l], 8, D, stg, 512)
                xm, bxm = xms.next()
                for n in range(8):
                    ps, bp = pso.next()
                    for k in range(8):
                        P.mm(ps[:, :W], Wo[:, k, n * 128 : (n + 1) * 128], yt[:, k, :W], k == 0, k == 7,
                             [wb[(k, n // 4)], by], [bp])
                    P.stt("dve", xm[:, n, :W], ps[:, :W], self.mod(2, n, col), xt[:, n, :W], ALU.mult, ALU.add,
                          [bp, bx, self.bmod], [bxm])
                P.dma(fm(self.resT)[:, :, c0 : c0 + W], xm[:, :, :W], r=[bxm], q="pool")
                hT, bh = hts.next()
                self.norm_mod(nt, xm, bxm, W, self.A2, self.bA2, 3, col, hT, bh)
                P.dma(fm(self.hfT)[:, :, c0 : c0 + W], hT[:, :, :W], r=[bh], q="pool")

    def tiles_C2(self, last):
        tl = []
        if not last:
            tl.append((0, CTX, 0, CTX, 1))
        step = 456
        s = 0
        while s < SEQ:
            e = min(SEQ, s + step)
            tl.append((CTX, SEQ, s, e, 0))
            s = e
        return tl

    def phase_C2(self, l, last):
        P = self.P
        with P.scope():
            Wu, _ = P.tile("Wup", [128, 8, 2 * DFF], BF16)
            Wd, _ = P.tile("Wdn", [128, 22, D], BF16)
            stg = Rot(P, "wst", 3, [128, 1024], F32)
            wbu = wbd = None
            self.lc_setup()
            cw, bcw = P.tile("cw", [128, 3, 44], F32)
            for j in range(3):
                self.load_cols(cw[:, j, :], bcw, self.w["ffn_conv"][l][j].rearrange("(c p) -> c p", p=128), 44)
            hin = Rot(P, "hin", 2, [128, 8, 458], BF16)
            xin = Rot(P, "xf", 2, [128, 456], F32)
            xos = Rot(P, "xo2", 2, [128, 456], F32)
            gts = Rot(P, "gT", 1, [128, 22, 456], BF16)
            psa = Rot(P, "psa", 2, [128, 458], F32, psum=True)
            psb = Rot(P, "psb", 2, [128, 458], F32, psum=True)
            psd = Rot(P, "psd", 2, [128, 456], F32, psum=True)
            cas = Rot(P, "ca", 2, [128, 456], F32)
            cbs = Rot(P, "cb", 2, [128, 456], F32)
            sas = Rot(P, "sa", 2, [128, 456], F32)
            for (so, sl, s, e, col) in self.tiles_C2(last):
                Wo = e - s
                W = Wo + 2
                lo, hi = s - 1, e + 1
                ht, bh = hin.next()
                a0 = 1 if lo < 0 else 0
                a1 = W - 1 if hi > sl else W
                P.dma(ht[:, :, a0:a1], fm(self.hfT)[:, :, so + lo + a0 : so + lo + a1], w=[bh])
                if a0:
                    P.memset("pool", ht[:, :, 0:1], 0.0, [bh])
                if a1 < W:
                    P.memset("pool", ht[:, :, W - 1 : W], 0.0, [bh])
                if wbu is None:
                    wbu = {}
                    for blk in range(0, DFF, 704):
                        for half in range(2):
                            c0w = half * DFF + blk
                            sub = self.load_w_bf16(Wu[:, :, c0w : c0w + 704], self.w["w_up"][l][:, c0w : c0w + 704], 8, 704, stg, 704)
                            for (k, _c), b_ in sub.items():
                                wbu[(k, c0w // 704)] = b_
                    wbd = self.load_w_bf16(Wd, self.w["w_down"][l], 22, D, stg, 1024, order="row")
                gT, bg = gts.next()
                for f in range(22):
                    halves = []
                    for (half, psr, cs) in ((0, psa, cas), (1, psb, cbs)):
                        ch = half * 22 + f
                        ps, bp = psr.next()
                        for k in range(8):
                            P.mm(ps[:, :W], Wu[:, k, ch * 128 : (ch + 1) * 128], ht[:, k, :W], k == 0, k == 7,
                                 [wbu[(k, (ch * 128) // 704)], wbu[(k, (ch * 128 + 127) // 704)], bh], [bp])
                        ct, bc = cs.next()
                        P.actf(ct[:, :Wo], ps[:, 1 : W - 1], AF.Identity, [bp, bcw], [bc], scale=cw[:, 1, ch : ch + 1])
                        P.stt("dve", ct[:, :Wo], ps[:, 0 : W - 2], cw[:, 0, ch : ch + 1], ct[:, :Wo], ALU.mult, ALU.add,
                              [bp, bcw, bc], [bc])
                        P.stt("dve", ct[:, :Wo], ps[:, 2:W], cw[:, 2, ch : ch + 1], ct[:, :Wo], ALU.mult, ALU.add,
                              [bp, bcw, bc], [bc])
                        halves.append((ct, bc))
                    (ca, bca), (cb, bcb) = halves
                    sa, bsa = sas.next()
                    P.actf(sa[:, :Wo], ca[:, :Wo], AF.Silu, [bca], [bsa])
                    P.tt("pool", gT[:, f, :Wo], sa[:, :Wo], cb[:, :Wo], ALU.mult, [bsa, bcb], [bg])
                for n in range(8):
                    xt, bx = xin.next()
                    P.dma(xt[:, :Wo], self.resT[n, :, so + s : so + e], w=[bx])
                    ps, bp = psd.next()
                    for f in range(22):
                        P.mm(ps[:, :Wo], Wd[:, f, n * 128 : (n + 1) * 128], gT[:, f, :Wo], f == 0, f == 21,
                             [wbd[(f, 0)], bg], [bp])
                    xo, bxo = xos.next()
                    P.stt("dve", xo[:, :Wo], ps[:, :Wo], self.mod(5, n, col), xt[:, :Wo], ALU.mult, ALU.add,
                          [bp, bx, self.bmod], [bxo])
                    P.dma(self.resT[n, :, so + s : so + e], xo[:, :Wo], r=[bxo], q="pool")

    def phase_final(self):
        P = self.P
        ident, bi = self.k["ident_f"]
        with P.scope():
            xin = Rot(P, "fin", 2, [128, 8, 128], F32)
            pst = Rot(P, "psf", 2, [128, 1024], F32, psum=True)
            xo = Rot(P, "fout", 2, [128, 1024], F32)
            for s in range(SEQ // 128):
                t0 = CTX + s * 128
                xt, bx = xin.next()
                P.dma(xt[:], fm(self.resT)[:, :, t0 : t0 + 128], w=[bx])
                pt, bp = pst.next()
                for k in range(8):
                    P.tr(pt[:, k * 128 : (k + 1) * 128], xt[:, k, :], ident[:], [bx, bi], [bp])
                ot, bo = xo.next()
                P.cp("act" if s % 2 else "dve", ot[:], pt[:], [bp], [bo])
                P.dma(self.out[s * 128 : (s + 1) * 128, :], ot[:], r=[bo], q="pool", is_out=True)

    def build(self):
        P = self.P
        cfg = self.cfg
        self.ada_done = set()
        self.lc_in = None
        self.load_consts()
        self.gab, self.bgab = P.tile("gab", [128, NT // 128, 16], F32)
        layers = cfg.get("layers", list(range(DEPTH)))
        phases = cfg.get("phases", "0aABCF")
        if "0" in phases:
            self.phase_init()
        for l in layers:
            last = (l == DEPTH - 1) or cfg.get("force_last", False)
            self.set_layer(l)
            if "a" in phases and l not in self.ada_done:
                self.phase_ada(l)
            if "A" in phases:
                self.phase_A(l)
            if "B" in phases:
                self.phase_B(l, last)
            if "C" in phases:
                self.phase_C1(l, last)
                self.phase_C2(l, last)
        if "F" in phases:
            self.phase_final()
        P.emit()
        return self.nc

    def phase_B(self, l, last):
        sub = self.cfg.get("mixers", "png")
        if "p" in sub:
            self.phase_pool(l, last)
        if "n" in sub:
            self.phase_na(l, last)
        if "g" in sub:
            self.phase_gdn(l, last)

    def phase_pool(self, l, last):
        P = self.P
        with P.scope():
            gens = [self.pool_gen(l, last)]
            if self.cfg.get("ada_overlap", True) and (l + 1) in self.cfg.get("layers", list(range(DEPTH))):
                gens.append(self.ada_gen(l + 1))
                self.ada_done.add(l + 1)
            interleave(gens)

    def pool_gen(self, l, last):
        P = self.P
        if True:
            pwf, bpf = P.tile("pwf", [128, 2, 128], F32)
            pw, bpw = P.tile("pw", [128, 2, 128], BF16)
            P.memset("pool", pwf[:], 0.0, [bpf])
            for g in range(4):
                h = 64 * (g % 2)
                P.dma(pwf[h : h + 64, g // 2, h : h + 64], self.w["pool_w"][l][g], w=[bpf])
            P.cp("pool", pw[:], pwf[:], [bpf], [bpw])
            self.lc_setup()
            sc, bsc = P.tile("psc", [128, 2], F32)
            self.load_cols(sc[:], bsc, self.w["pool_scale"][l].rearrange("(c p) -> c p", p=128), 2)
            xbs = Rot(P, "pxb", 2, [128, 2, 528], BF16)
            ics = Rot(P, "pic", 2, [128, 2, 512], F32)
            lv = [P.tile(f"plv{i}", [128, 2, 528], F32) for i in range(4)]
            mt, bm = P.tile("pm", [128, 2, 512], F32)
            pls = Rot(P, "ppl", 2, [128, 2, 512], BF16)
            pso = Rot(P, "pps", 2, [128, 512], F32, psum=True)
            yas = Rot(P, "pya", 2, [128, 2, 512], BF16)
            seqs = [(CTX, SEQ, "invc_lat")]
            if not last:
                seqs.append((0, CTX, "invc_ctx"))
            for (so, sl, icn) in seqs:
                for s0 in range(0, sl, 512):
                    Wo = min(512, sl - s0)
                    W = Wo + 16
                    xb, bx = xbs.next()
                    a0 = max(0, 8 - s0)
                    a1 = min(W, sl - s0 + 8)
                    P.dma(xb[:, :, a0:a1], fm(self.pF)[:, 0:2, so + s0 - 8 + a0 : so + s0 - 8 + a1], w=[bx])
                    if a0 > 0:
                        P.memset("pool", xb[:, :, 0:a0], 0.0, [bx])
                    if a1 < W:
                        P.memset("pool", xb[:, :, a1:W], 0.0, [bx])
                    ic, bic = ics.next()
                    P.dma(ic[:, :, :Wo], fm(self.big[icn])[:, :, s0 : s0 + Wo], w=[bic])
                    (s2, b2), (s4, b4), (s8, b8), (s16, b16) = lv
                    P.tt("pool", s2[:, :, 1:W], xb[:, :, 1:W], xb[:, :, 0 : W - 1], ALU.add, [bx], [b2])
                    P.tt("pool", s4[:, :, 3:W], s2[:, :, 3:W], s2[:, :, 1 : W - 2], ALU.add, [b2], [b4])
                    P.tt("pool", s8[:, :, 7:W], s4[:, :, 7:W], s4[:, :, 3 : W - 4], ALU.add, [b4], [b8])
                    P.tt("pool", s16[:, :, 15:W], s8[:, :, 15:W], s8[:, :, 7 : W - 8], ALU.add, [b8], [b16])
                    for (c, h, src, bs_, sh) in ((0, 0, s2, b2, 0), (0, 1, s4, b4, 1), (1, 0, s8, b8, 3), (1, 1, s16, b16, 7)):
                        pr = slice(64 * h, 64 * h + 64)
                        P.tt("pool", mt[pr, c, :Wo], src[pr, c, 8 + sh : 8 + sh + Wo], ic[pr, c, :Wo], ALU.mult,
                             [bs_, bic], [bm])
                    pl, bpl = pls.next()
                    P.tt("pool", pl[:, :, :Wo], mt[:, :, :Wo], xb[:, :, 8 : 8 + Wo], ALU.subtract, [bm, bx], [bpl])
                    ya, bya = yas.next()
                    for c in range(2):
                        ps, bp = pso.next()
                        P.mm(ps[:, :Wo], pw[:, c, :], pl[:, c, :Wo], True, True, [bpw, bpl], [bp])
                        P.ts("dve", ya[:, c, :Wo], ps[:, :Wo], sc[:, c : c + 1], None, ALU.mult, None, [bp, bsc], [bya])
                    P.dma(fm(self.yT)[:, 0:2, so + s0 : so + s0 + Wo], ya[:, :, :Wo], r=[bya], q="act")
                    yield

    def phase_na(self, l, last):
        P = self.P
        bd, bbd = self.k["bd_b"]
        with P.scope():
            qk, bqk = P.tile("naqk", [128, 4, NT], BF16)
            vp, bvp = P.tile("navp", [128, NT // 128, 4, 128], BF16)
            gn, bgn = P.tile("nagn", [128, 2], F32)
            with P.scope():
                self.lc_setup()
                ti, bti = self.lc_in.next()
                for j, nm in enumerate(("na_q_norm", "na_k_norm")):
                    for hh in range(2):
                        P.dma(ti[j : j + 1, 64 * hh : 64 * hh + 64], self.w[nm][l].rearrange("(o d) -> o d", o=1), w=[bti])
                pt, bp = self.lc_ps
                ident, bi = self.k["ident_f"]
                P.tr(pt[:, :2], ti[:2, :], ident[:2, :2], [bti, bi], [bp])
                P.cp("dve", gn[:], pt[:, :2], [bp], [bgn])
                P.ts("dve", gn[:, 1:2], gn[:, 1:2], 8.0, None, ALU.mult, None, [bgn], [bgn])
            P.memset("pool", vp[:], 0.0, [bvp])
            for h in range(4):
                o = 64 * (h % 2)
                P.dma(vp[:, :, h, o : o + 64],
                      self.pT[:, 512 + 64 * h : 512 + 64 * h + 64].rearrange("(t p) c -> p t c", p=128), w=[bvp])
            with P.scope():
                xbs = Rot(P, "nxb", 8, [128, 512], BF16)
                sqs = Rot(P, "nsq", 8, [128, 512], BF16)
                rss = Rot(P, "nrs", 8, [128, 512], F32)
                pss = FreeList(P, "nps", 7, [128, 512], F32, psum=True)

                def nchunk(c, t0):
                    def run():
                        W = min(512, NT - t0)
                        xb, bx = xbs.next()
                        P.dma(xb[:, :W], self.pF[14 + c, :, t0 : t0 + W], w=[bx])
                        sq, bsq = sqs.next()
                        P.actf(sq[:, :W], xb[:, :W], AF.Square, [bx], [bsq])
                        psI = pss.alloc()
                        ps, bps = psI
                        P.mm(ps[:, :W], bd[:], sq[:, :W], True, True, [bbd, bsq], [bps])
                        yield
                        rs, brs = rss.next()
                        P.actf(rs[:, :W], ps[:, :W], AF.Sqrt, [bps], [brs], scale=1.0, bias=64.0 * EPS)
                        pss.release(psI)
                        yield
                        P.recip(rs[:, :W], rs[:, :W], [brs], [brs])
                        yield
                        P.stt("dve", qk[:, c, t0 : t0 + W], xb[:, :W], gn[:, c // 2 : c // 2 + 1], rs[:, :W],
                              ALU.mult, ALU.mult, [bx, bgn, brs], [bqk])
                    return run

                rolling([nchunk(c, t0) for c in range(4) for t0 in range(0, NT, 512)], 6, stagger=1)
            olo, bol = self.k["ones_lo"]
            ohi, boh = self.k["ones_hi"]
            bis = Rot(P, "nab", 2, [128, 4, 5, 128], F32)
            pS = Rot(P, "naS", 2, [128, 7, 128], F32, psum=True)
            pO = Rot(P, "naO", 2, [128, 512], F32, psum=True)
            pD = Rot(P, "naD", 2, [128, 512], F32, psum=True)
            sbs = Rot(P, "nas", 4, [128, 5, 128], F32)
            pts = Rot(P, "nap", 6, [128, 7, 128], BF16)
            rds = Rot(P, "nar", 4, [128, 128], F32)
            ycs = Rot(P, "nay", 4, [128, 2, 128], BF16)
            groups = []
            for R in range(32):
                ch = na_chunks(R)
                groups.append((CTX + 128 * R, [(CTX + 128 * m, 2 + m) for m in ch], NA_CLASSES.get(R, 0)))
            if not last:
                for Rc in range(2):
                    groups.append((128 * Rc, [], None))
            cur = {"cls": None, "bt": None, "bbt": None}

            def group(q0, band, cls):
                if cls is not None and cls != cur["cls"]:
                    cur["bt"], cur["bbt"] = bis.next()
                    P.dma(cur["bt"][:], self.big["na_bias"][l, cls].rearrange("p (h s q) -> p h s q", h=4, s=5), w=[cur["bbt"]])
                    cur["cls"] = cls
                bt, bbt = cur["bt"], cur["bbt"]
                ns = len(band)
                keys = band + [(0, 0), (128, 1)]
                slots = list(range(ns)) + [5, 6]
                yc, byc = ycs.next()
                for c in range(2):
                    pps = []
                    for hh in range(2):
                        h = 2 * c + hh
                        pr = slice(64 * hh, 64 * hh + 64)
                        ps, bps = pS.next()
                        for sl_, (k0, vt) in zip(slots, keys):
                            P.mm(ps[:, sl_, :], qk[pr, 2 + c, k0 : k0 + 128], qk[pr, c, q0 : q0 + 128], True, True,
                                 [bqk], [bps])
                        yield
                        pp, bpp = pts.next()
                        if ns:
                            sb_, bsb = sbs.next()
                            P.tt("dve", sb_[:, :ns, :], ps[:, :ns, :], bt[:, h, :ns, :], ALU.add, [bps, bbt], [bsb])
                            P.actf(pp[:, :ns, :], sb_[:, :ns, :], AF.Exp, [bsb], [bpp])
                        P.actf(pp[:, 5:7, :], ps[:, 5:7, :], AF.Exp, [bps], [bpp])
                        pps.append((pp, bpp))
                        yield
                    po, bpo = pO.next()
                    pd, bpd = pD.next()
                    nmm = 2 * len(keys)
                    imm = 0
                    for hh in range(2):
                        h = 2 * c + hh
                        pp, bpp = pps[hh]
                        for sl_, (k0, vt) in zip(slots, keys):
                            P.mm(po[:, 0:128], vp[:, vt, h, :], pp[:, sl_, :], imm == 0, imm == nmm - 1, [bvp, bpp], [bpo])
                            imm += 1
                    imm = 0
                    for hh in range(2):
                        pp, bpp = pps[hh]
                        onesp, bon = (olo, bol) if hh == 0 else (ohi, boh)
                        for sl_, (k0, vt) in zip(slots, keys):
                            P.mm(pd[:, 0:128], onesp[:], pp[:, sl_, :], imm == 0, imm == nmm - 1, [bon, bpp], [bpd])
                            imm += 1
                    yield
                    rd, brd = rds.next()
                    P.recip(rd[:], pd[:, 0:128], [bpd], [brd])
                    P.tt("dve", yc[:, c, :], po[:, 0:128], rd[:], ALU.mult, [bpo, brd], [byc])
                    yield
                P.dma(fm(self.yT)[:, 6:8, q0 : q0 + 128], yc[:], r=[byc], q="pool")

            for i in range(0, len(groups), 2):
                interleave([group(*g) for g in groups[i : i + 2]])

    def phase_gdn(self, l, last):
        P = self.P
        ident_f, bif = self.k["ident_f"]
        ident_b, bib = self.k["ident_b"]
        ones_b, bob = self.k["ones_b"]
        ones_f, bof = self.k["ones_f"]
        offd, bofd = self.k["offd"]
        rperm, brp = self.k["rperm"]
        NTL = NT // 128
        with P.scope():
            gcol, bg = P.tile("g_col", [128, NTL, 8], F32)
            beta, bbe = P.tile("g_beta", [128, NTL, 8], F32)
            nbeta, bnb = P.tile("g_nbeta", [128, NTL, 8], F32)
            cst8, bc8 = P.tile("g_c8", [128, 2, 8], F32)
            P.dma(cst8[:, 0, :], self.w["gdn_dt_bias"][l].rearrange("d h -> (d h)").partition_broadcast(128), w=[bc8])
            P.dma(cst8[:, 1, :], self.w["gdn_a_log"][l].rearrange("d h -> (d h)").partition_broadcast(128), w=[bc8])
            P.actf(cst8[:, 1, :], cst8[:, 1, :], AF.Exp, [bc8], [bc8])
            P.ts("dve", cst8[:, 1, :], cst8[:, 1, :], -1.0, None, ALU.mult, None, [bc8], [bc8])
            P.tt("dve", gcol[:], self.gab[:, :, 0:8], cst8[:, 0:1, :].to_broadcast([128, NTL, 8]), ALU.add,
                 [self.bgab, bc8], [bg])
            P.actf(gcol[:], gcol[:], AF.Exp, [bg], [bg])
            P.actf(gcol[:], gcol[:], AF.Ln, [bg], [bg], scale=1.0, bias=1.0)
            P.tt("dve", gcol[:], gcol[:], cst8[:, 1:2, :].to_broadcast([128, NTL, 8]), ALU.mult, [bg, bc8], [bg])
            P.actf(beta[:], self.gab[:, :, 8:16], AF.Sigmoid, [self.bgab], [bbe])
            P.ts("dve", nbeta[:], beta[:], -1.0, None, ALU.mult, None, [bbe], [bnb])
            parts = self.cfg.get("gdn_parts", "pso")
            if "p" in parts:
                self.gdn_prep(l)
            if "s" in parts:
                self.gdn_scan(l, last, gcol, bg, beta, bbe, nbeta, bnb)
            if "o" in parts:
                self.gdn_out(l, last)

    def gdn_prep(self, l):
        P = self.P
        ident_b, bib = self.k["ident_b"]
        ones_b, bob = self.k["ones_b"]
        rperm, brp = self.k["rperm"]
        with P.scope():
            cwt, bcw = P.tile("gcw", [128, 5, 12], F32)
            with P.scope():
                self.lc_setup()
                for j in range(5):
                    self.load_cols(cwt[:, j, :], bcw, self.w["gdn_conv"][l][j].rearrange("(c p) -> c p", p=128), 12)
            dw, bdw = P.tile("gdw", [128, 12, 5, 128], BF16)
            for ch in range(12):
                for j in range(5):
                    P.ts("pool" if (ch + j) % 2 else "dve", dw[:, ch, j, :], ident_b[:], cwt[:, j, ch : ch + 1], None,
                         ALU.mult, None, [bib, bcw], [bdw])
            xbs = Rot(P, "gxb", 9, [128, 516], BF16)
            sxs = Rot(P, "gsx", 8, [128, 512], F32)
            sqs = Rot(P, "gsq", 8, [128, 512], BF16)
            rss = Rot(P, "grs", 8, [128, 512], F32)
            xns = Rot(P, "gxn", 16, [128, 512], BF16)
            t1s = Rot(P, "gt1", 8, [128, 512], F32)
            t2s = Rot(P, "gt2", 8, [128, 512], F32)
            cst = Rot(P, "gcs", 3, [128, 2, 512], F32)
            tks = Rot(P, "gtk", 3, [128, 4, 128], BF16)
            psC = psA = psB = FreeList(P, "gpC", 7, [128, 512], F32, psum=True)
            psT = Rot(P, "gpT", 1, [128, 4, 128], BF16, psum=True)

            def chunk(so, sl, s0, Wo, rope, cs_, bcs, ch):
                W = Wo + 4
                c0 = so + s0
                kind, h = ch // 4, ch % 4
                xb, bx = xbs.next()
                a0 = max(0, 2 - s0)
                a1 = min(W, sl - s0 + 2)
                P.dma(xb[:, a0:a1], self.pF[2 + ch, :, c0 - 2 + a0 : c0 - 2 + a1], w=[bx])
                if a0 > 0:
                    P.memset("pool", xb[:, 0:a0], 0.0, [bx])
                if a1 < W:
                    P.memset("pool", xb[:, a1:W], 0.0, [bx])
                pcI = psC.alloc()
                pc, bpc = pcI
                for j in range(5):
                    P.mm(pc[:, :Wo], dw[:, ch, j, :], xb[:, j : j + Wo], j == 0, j == 4, [bdw, bx], [bpc])
                yield
                xn, bxn = xns.next()
                if kind == 2:
                    P.actf(xn[:, :Wo], pc[:, :Wo], AF.Silu, [bpc], [bxn])
                    psC.release(pcI)
                else:
                    sx, bsx = sxs.next()
                    P.actf(sx[:, :Wo], pc[:, :Wo], AF.Silu, [bpc], [bsx])
                    psC.release(pcI)
                    sq, bsq = sqs.next()
                    P.actf(sq[:, :Wo], sx[:, :Wo], AF.Square, [bsx], [bsq])
                    psI = psA.alloc()
                    ps, bps = psI
                    P.mm(ps[:, :Wo], ones_b[:], sq[:, :Wo], True, True, [bob, bsq], [bps])
                    yield
                    rs, brs = rss.next()
                    P.actf(rs[:, :Wo], ps[:, :Wo], AF.Sqrt, [bps], [brs], scale=1.0, bias=EPS)
                    psA.release(psI)
                    P.recip(rs[:, :Wo], rs[:, :Wo], [brs], [brs])
                    qs = 128.0 ** -0.5 if kind == 0 else 1.0
                    if not rope:
                        P.stt("dve", xn[:, :Wo], sx[:, :Wo], qs, rs[:, :Wo], ALU.mult, ALU.mult, [bsx, brs], [bxn])
                    else:
                        xm, bxm = xns.next()
                        P.stt("dve", xm[:, :Wo], sx[:, :Wo], qs, rs[:, :Wo], ALU.mult, ALU.mult, [bsx, brs], [bxm])
                        prI = psB.alloc()
                        pr, bpr = prI
                        P.mm(pr[:, :Wo], rperm[:], xm[:, :Wo], True, True, [brp, bxm], [bpr])
                        yield
                        t1, bt1 = t1s.next()
                        t2, bt2 = t2s.next()
                        P.tt("pool", t1[:, :Wo], xm[:, :Wo], cs_[:, 0, :Wo], ALU.mult, [bxm, bcs], [bt1])
                        P.tt("dve", t2[:, :Wo], pr[:, :Wo], cs_[:, 1, :Wo], ALU.mult, [bpr, bcs], [bt2])
                        psB.release(prI)
                        P.tt("pool", xn[:, :Wo], t1[:, :Wo], t2[:, :Wo], ALU.add, [bt1, bt2], [bxn])
                yield
                if kind == 0:
                    P.dma(self.gQT[h, :, c0 : c0 + Wo], xn[:, :Wo], r=[bxn], q="pool")
                if kind == 1:
                    P.dma(self.gKT[h, :, c0 : c0 + Wo], xn[:, :Wo], r=[bxn], q="pool")
                if kind >= 1:
                    dst = self.gK if kind == 1 else self.gV
                    nb = Wo // 128
                    pt, bpt = psT.next()
                    for b in range(nb):
                        P.tr(pt[:, b, :], xn[:, b * 128 : (b + 1) * 128], ident_b[:], [bxn, bib], [bpt])
                    tk, btk = tks.next()
                    P.cp("act", tk[:, :nb, :], pt[:, :nb, :], [bpt], [btk])
                    P.dma(dst[c0 : c0 + Wo, h * 128 : (h + 1) * 128].rearrange("(b p) d -> p b d", p=128),
                          tk[:, :nb, :], r=[btk], q="act")

            tasks = []
            tabs = {}

            def mk(so, sl, s0, Wo, rope, ch):
                def run():
                    cs_ = bcs = None
                    if rope:
                        if s0 not in tabs:
                            cs_, bcs = cst.next()
                            P.dma(cs_[:, 0, :Wo], self.big["rope_cos"][:, s0 : s0 + Wo], w=[bcs])
                            P.dma(cs_[:, 1, :Wo], self.big["rope_sin"][:, s0 : s0 + Wo], w=[bcs])
                            tabs[s0] = (cs_, bcs)
                        cs_, bcs = tabs[s0]
                    yield from chunk(so, sl, s0, Wo, rope, cs_, bcs, ch)
                return run

            for (so, sl, rope) in ((0, CTX, False), (CTX, SEQ, True)):
                for s0 in range(0, sl, 512):
                    Wo = min(512, sl - s0)
                    for ch in range(12):
                        tasks.append(mk(so, sl, s0, Wo, rope, ch))
            rolling(tasks, 6)

    def gdn_scan(self, l, last, gcol, bg, beta, bbe, nbeta, bnb):
        P = self.P
        ident_b, bib = self.k["ident_b"]
        ident_f, bif = self.k["ident_f"]
        ones_f, bof = self.k["ones_f"]
        offd, bofd = self.k["offd"]
        NTL = NT // 128
        with P.scope():
            tri, btri = P.tile("gtri", [128, 2, 128], F32)
            negm, bnm = P.tile("gnegm", [128, 2, 128], F32)
            P.dma(tri[:], self.big["tri"].rearrange("d p c -> p d c"), w=[btri])
            P.dma(negm[:], self.big["negm"].rearrange("d p c -> p d c"), w=[bnm])
            R3 = lambda nm, dt, n=3: Rot(P, nm, n, [128, 4, 128], dt)
            NL = 3
            RD = []
            for d in range(NL):
                t = f"{d}"
                RD.append(dict(
                    pg=Rot(P, "gpg" + t, 2, [128, 4, 128], F32, psum=True),
                    q=R3("gq" + t, BF16, 1), k=R3("gk" + t, BF16, 1), km=R3("gkm" + t, BF16, 1), vm=R3("gvm" + t, BF16, 1),
                    rg=R3("grg" + t, F32, 1), dt=R3("gdt" + t, F32, 1), ea=R3("gea" + t, F32, 1), es=R3("ges" + t, F32, 1),
                    egr=R3("geg" + t, F32, 1),
                    m=R3("gm" + t, F32, 2), mt=R3("gmt" + t, F32, 2), pk=R3("gp" + t, F32, 2), tt=R3("gtt" + t, BF16, 1),
                    it=R3("git" + t, BF16, 3), wt=R3("gwt" + t, BF16, 3), qd=R3("gqd" + t, BF16, 3), kd=R3("gkd" + t, BF16, 3),
                    ke=R3("gke" + t, BF16, 1), ub=R3("gub" + t, F32, 3),
                    sm=Rot(P, "gsm" + t, 3, [128, 4, 4], F32),
                ))
            RS = [dict(vn=R3(f"gvn{d}", BF16, 1), ot=R3(f"got{d}", F32, 2)) for d in range(2)]
            pgs = Rot(P, "gpgs", 2, [128, 4, 128], F32, psum=True)
            S = [P.tile(f"gS{d}", [128, 4, 128], F32) for d in range(2)]
            Sb = [P.tile(f"gSb{d}", [128, 4, 128], BF16) for d in range(2)]
            for d in range(2):
                P.memset("pool", S[d][0][:], 0.0, [S[d][1]])
                P.memset("pool", Sb[d][0][:], 0.0, [Sb[d][1]])
            order = [list(range(NTL)), [1, 0] + list(range(NTL - 1, 1, -1))]
            evi = [0]

            def evac(out, in_, r, w):
                e = "act" if evi[0] % 2 == 0 else "dve"
                evi[0] += 1
                P.cp(e, out, in_, r, w)

            def pre(n, d, res, lane):
                R = RD[lane]
                pg = R["pg"]
                lastc = 127 if d == 0 else 0
                hs = slice(d * 4, d * 4 + 4)
                qt, bq = R["q"].next()
                kt, bk = R["k"].next()
                km, bkm = R["km"].next()
                vm, bvm = R["vm"].next()
                cs = slice(n * 128, (n + 1) * 128)
                P.dma(qt[:], self.gQT[:, :, cs].rearrange("h p t -> p h t"), w=[bq])
                P.dma(kt[:], self.gKT[:, :, cs].rearrange("h p t -> p h t"), w=[bk])
                P.dma(km[:], self.gK[cs, :].rearrange("p (h d) -> p h d", h=4), w=[bkm])
                P.dma(vm[:], self.gV[cs, :].rearrange("p (h d) -> p h d", h=4), w=[bvm])
                g4 = gcol[:, n, hs]
                rg, brg = R["rg"].next()
                P.tt("pool", rg[:], tri[:, d : d + 1, :].to_broadcast([128, 4, 128]), g4.unsqueeze(2).to_broadcast([128, 4, 128]),
                     ALU.mult, [btri, bg], [brg])
                gcr, bgcr = pg.next()
                P.mm(gcr[:].rearrange("p h c -> p (h c)"), ones_f[:], rg[:].rearrange("p h c -> p (h c)"), True, True,
                     [bof, brg], [bgcr])
                gcc, bgcc = pg.next()
                P.mm(gcc[:, 0, 0:4], tri[:, d, :], g4, True, True, [btri, bg], [bgcc])
                yield
                sm, bsm = R["sm"].next()
                gcs, egc, kdsc, glast = sm[:, 0, :], sm[:, 1, :], sm[:, 2, :], sm[:, 3, :]
                P.cp("dve", gcs, gcc[:, 0, 0:4], [bgcc], [bsm])
                P.actf(egc, gcs, AF.Exp, [bsm], [bsm])
                P.tt("dve", kdsc, gcr[:, :, lastc], gcs, ALU.subtract, [bgcr, bsm], [bsm])
                P.actf(kdsc, kdsc, AF.Exp, [bsm], [bsm])
                P.actf(glast, gcr[:, :, lastc], AF.Exp, [bgcr], [bsm])
                yield
                egr, begr = R["egr"].next()
                P.actf(egr[:], gcr[:], AF.Exp, [bgcr], [begr])
                dt_, bdt = R["dt"].next()
                P.tt("dve", dt_[:], gcr[:], gcs.unsqueeze(2).to_broadcast([128, 4, 128]), ALU.subtract, [bgcr, bsm], [bdt])
                P.tt("dve", dt_[:], dt_[:], negm[:, d : d + 1, :].to_broadcast([128, 4, 128]), ALU.min, [bdt, bnm], [bdt])
                kk, bkk = pg.next()
                for h in range(4):
                    P.mm(kk[:, h, :], kt[:, h, :], kt[:, h, :], True, True, [bk], [bkk])
                yield
                ea, bea = R["ea"].next()
                P.actf(ea[:], dt_[:], AF.Exp, [bdt], [bea])
                es, bes = R["es"].next()
                P.tt("pool", es[:], ea[:], offd[:].unsqueeze(1).to_broadcast([128, 4, 128]), ALU.mult, [bea, bofd], [bes])
                yield
                m, bm = R["m"].next()
                for h in range(4):
                    P.stt("dve", m[:, h, :], kk[:, h, :], nbeta[:, n, d * 4 + h : d * 4 + h + 1], es[:, h, :], ALU.mult, ALU.mult,
                          [bkk, bnb, bes], [bm])
                kq, bkq = pg.next()
                for h in range(4):
                    P.mm(kq[:, h, :], kt[:, h, :], qt[:, h, :], True, True, [bk, bq], [bkq])
                yield
                it, bit = R["it"].next()
                P.tt("dve", it[:], kq[:], ea[:], ALU.mult, [bkq, bea], [bit])
                ptr, bptr = pg.next()
                for h in range(4):
                    P.tr(ptr[:, h, :], m[:, h, :], ident_f[:], [bm, bif], [bptr])
                pk, bpk = R["pk"].next()
                P.tt("pool", pk[:], m[:], ident_f[:].unsqueeze(1).to_broadcast([128, 4, 128]), ALU.add, [bm, bif], [bpk])
                yield
                mt, bmt = R["mt"].next()
                P.cp("act", mt[:], ptr[:], [bptr], [bmt])
                yield
                for lev in range(6):
                    if lev < 5:
                        pm, bpm = pg.next()
                        for h in range(4):
                            P.mm(pm[:, h, :], mt[:, h, :], m[:, h, :], True, True, [bmt, bm], [bpm])
                        yield
                        m2, bm2 = R["m"].next()
                        evac(m2[:], pm[:], [bpm], [bm2])
                        yield
                        pmt, bpmt = pg.next()
                        for h in range(4):
                            P.tr(pmt[:, h, :], m2[:, h, :], ident_f[:], [bm2, bif], [bpmt])
                    else:
                        pmt, bpmt = pg.next()
                        for h in range(4):
                            P.mm(pmt[:, h, :], m[:, h, :], mt[:, h, :], True, True, [bmt, bm], [bpmt])
                    yield
                    mt2, bmt2 = R["mt"].next()
                    P.cp("act", mt2[:], pmt[:], [bpmt], [bmt2])
                    yield
                    pp, bpp = pg.next()
                    for h in range(4):
                        P.mm(pp[:, h, :], mt2[:, h, :], pk[:, h, :], True, True, [bmt2, bpk], [bpp])
                    yield
                    pk2, bpk2 = R["pk"].next()
                    P.tt("dve", pk2[:], pp[:], pk[:], ALU.add, [bpp, bpk], [bpk2])
                    pk, bpk = pk2, bpk2
                    mt, bmt = mt2, bmt2
                    if lev < 5:
                        m, bm = m2, bm2
                    yield
                tt_, btt = R["tt"].next()
                P.cp("act", tt_[:], pk[:], [bpk], [btt])
                ke, bke = R["ke"].next()
                P.tt("pool", ke[:], km[:], egc.unsqueeze(2).to_broadcast([128, 4, 128]), ALU.mult, [bkm, bsm], [bke])
                yield
                up, bup = pg.next()
                for h in range(4):
                    P.mm(up[:, h, :], tt_[:, h, :], vm[:, h, :], True, True, [btt, bvm], [bup])
                wp, bwp = pg.next()
                for h in range(4):
                    P.mm(wp[:, h, :], ke[:, h, :], tt_[:, h, :], True, True, [bke, btt], [bwp])
                qd, bqd = R["qd"].next()
                P.tt("pool", qd[:], qt[:], egr[:], ALU.mult, [bq, begr], [bqd])
                kd, bkd = R["kd"].next()
                P.tt("pool", kd[:], km[:], kdsc.unsqueeze(2).to_broadcast([128, 4, 128]), ALU.mult, [bkm, bsm], [bkd])
                yield
                ub, bub = R["ub"].next()
                P.tt("dve", ub[:], up[:], beta[:, n, hs].unsqueeze(2).to_broadcast([128, 4, 128]), ALU.mult, [bup, bbe], [bub])
                wt, bwt = R["wt"].next()
                P.cp("act", wt[:], wp[:], [bwp], [bwt])
                res.update(n=n, d=d, it=(it, bit), wt=(wt, bwt), qd=(qd, bqd), kd=(kd, bkd), ub=(ub, bub), glast=(glast, bsm))

            def step(r):
                n, d = r["n"], r["d"]
                R = RS[d]
                s_, bs = S[d]
                sb_, bsb = Sb[d]
                it, bit = r["it"]
                wt, bwt = r["wt"]
                qd, bqd = r["qd"]
                kd, bkd = r["kd"]
                ub, bub = r["ub"]
                glast, bgl = r["glast"]
                ws, bws = pgs.next()
                for h in range(4):
                    P.mm(ws[:, h, :], wt[:, h, :], sb_[:, h, :], True, True, [bwt, bsb], [bws])
                yield
                vn, bvn = R["vn"].next()
                for h in range(4):
                    P.stt("dve", vn[:, h, :], ws[:, h, :], nbeta[:, n, d * 4 + h : d * 4 + h + 1], ub[:, h, :], ALU.mult, ALU.add,
                          [bws, bnb, bub], [bvn])
                P.tt("pool", s_[:], s_[:], glast.unsqueeze(2).to_broadcast([128, 4, 128]), ALU.mult, [bs, bgl], [bs])
                yield
                dsp, bds = pgs.next()
                for h in range(4):
                    P.mm(dsp[:, h, :], kd[:, h, :], vn[:, h, :], True, True, [bkd, bvn], [bds])
                yield
                P.tt("dve", s_[:], s_[:], dsp[:], ALU.add, [bs, bds], [bs])
                if not (last and n < 2):
                    op_, bop = pgs.next()
                    for h in range(4):
                        P.mm(op_[:, h, :], qd[:, h, :], sb_[:, h, :], True, False, [bqd, bsb], [bop])
                        P.mm(op_[:, h, :], it[:, h, :], vn[:, h, :], False, True, [bit, bvn], [bop])
                    yield
                    ot, bot = R["ot"].next()
                    P.cp("act", ot[:], op_[:], [bop], [bot])
                    P.dma(self.gO[d][n * 128 : (n + 1) * 128, :].rearrange("p (h d) -> p h d", h=4), ot[:], r=[bot], q="act")
                P.cp("act", sb_[:], s_[:], [bs], [bsb])

            done = {}
            stepped = set()

            def pre_lane(r):
                for _ in range(r * 14):
                    yield
                for j in range(r, 2 * NTL, NL):
                    i, d = j // 2, j % 2
                    while j - 2 * NL >= 0 and (j - 2 * NL) not in stepped:
                        yield
                    res = {}
                    yield from pre(order[d][i], d, res, r)
                    done[j] = res

            def step_lane(d):
                for i in range(NTL):
                    j = 2 * i + d
                    while j not in done:
                        yield
                    yield from step(done[j])
                    stepped.add(j)

            interleave([pre_lane(r) for r in range(NL)] + [step_lane(0), step_lane(1)])

    def gdn_out(self, l, last):
        P = self.P
        ident_b, bib = self.k["ident_b"]
        with P.scope():
            nw, bnw = P.tile("gnw", [128, 128], F32)
            P.dma(nw[:], self.w["gdn_norm"][l].partition_broadcast(128), w=[bnw])
            ofs = Rot(P, "gof", 5, [128, 4, 128], F32)
            obs = Rot(P, "gob", 5, [128, 4, 128], F32)
            zs = Rot(P, "gz", 5, [128, 4, 128], BF16)
            szs = Rot(P, "gsz", 5, [128, 4, 128], F32)
            sq2 = Rot(P, "gsq2", 5, [128, 4, 128], F32)
            sss = Rot(P, "gss", 5, [128, 4], F32)
            ybs = Rot(P, "gyb", 5, [128, 4, 128], BF16)
            yts = Rot(P, "gyt", 5, [128, 4, 128], BF16)
            psT = Rot(P, "gpT2", 4, [128, 4, 128], BF16, psum=True)
            def otile(n):
                rows = slice(n * 128, (n + 1) * 128)
                of, bof_ = ofs.next()
                ob, bob_ = obs.next()
                z, bz = zs.next()
                P.dma(of[:], self.gO[0][rows, :].rearrange("p (h d) -> p h d", h=4), w=[bof_])
                P.dma(ob[:], self.gO[1][rows, :].rearrange("p (h d) -> p h d", h=4), w=[bob_])
                P.dma(z[:], self.pT[rows, 0:512].rearrange("p (h d) -> p h d", h=4), w=[bz])
                P.tt("pool", of[:], of[:], ob[:], ALU.add, [bof_, bob_], [bof_])
                sq, bsq = sq2.next()
                P.tt("pool", sq[:], of[:], of[:], ALU.mult, [bof_], [bsq])
                ss, bss = sss.next()
                P._add("dve", lambda e, ss=ss, sq=sq: e.reduce_sum(out=ss[:], in_=sq[:], axis=AX.X), [bsq], [bss])
                yield
                P.actf(ss[:], ss[:], AF.Sqrt, [bss], [bss], scale=1.0 / 128.0, bias=EPS)
                P.recip(ss[:], ss[:], [bss], [bss])
                P.tt("dve", of[:], of[:], ss[:].unsqueeze(2).to_broadcast([128, 4, 128]), ALU.mult, [bof_, bss], [bof_])
                P.tt("pool", of[:], of[:], nw[:].unsqueeze(1).to_broadcast([128, 4, 128]), ALU.mult, [bof_, bnw], [bof_])
                yield
                sz, bsz = szs.next()
                P.actf(sz[:], z[:], AF.Silu, [bz], [bsz])
                yb, byb = ybs.next()
                P.tt("dve", yb[:], of[:], sz[:], ALU.mult, [bof_, bsz], [byb])
                pt, bpt = psT.next()
                for h in range(4):
                    P.tr(pt[:, h, :], yb[:, h, :], ident_b[:], [byb, bib], [bpt])
                yield
                yt, byt = yts.next()
                P.cp("act", yt[:], pt[:], [bpt], [byt])
                P.dma(fm(self.yT)[:, 2:6, n * 128 : (n + 1) * 128], yt[:], r=[byt], q="act")

            tl = list(range(2 if last else 0, NT // 128))
            for i in range(0, len(tl), 4):
                interleave([otile(n) for n in tl[i : i + 4]])


_CACHE = {}


def kernel(**inputs):
    cfg = {}
    if "full" not in _CACHE:
        _CACHE["full"] = Kern(cfg).build()
    nc = _CACHE["full"]
    consts = host_consts()
    rpb = np.asarray(inputs["na_rpb"], dtype=np.float32)
    nab = np.stack([na_bias_sets(rpb[l]) for l in range(DEPTH)]).reshape(DEPTH, 5, 128, 4 * 5 * 128)
    in_maps = []
    for b in range(8):
        m = {k: np.ascontiguousarray(inputs[k], dtype=np.float32) for k in W_SPECS}
        m["x"] = np.ascontiguousarray(inputs["x"][b], dtype=np.float32)
        m["ctx"] = np.ascontiguousarray(inputs["ctx"][b], dtype=np.float32)
        m["c"] = np.ascontiguousarray(inputs["c"][b], dtype=np.float32)
        m["c_ctx"] = np.ascontiguousarray(inputs["c_ctx"], dtype=np.float32)
        m.update(consts)
        m["na_bias"] = nab
        in_maps.append(m)

    res = run_bass_kernel_spmd(nc, in_maps, core_ids=list(range(8)))
    return np.stack([np.asarray(r["out"], dtype=np.float32) for r in res.results], axis=0)
```
